# Optimizing a Trainium2 kernel written in Bass

```python
import jax, jax.numpy as jnp
from jax import lax
import numpy as np

D_MODEL = 2048
BATCH = 2
SEQ = 8192
DEPTH = 4

N_MEM = 256
W_MEM = D_MODEL // 4
W_CONV = (D_MODEL - W_MEM) // 2
W_DIFF = D_MODEL - W_MEM - W_CONV
W_MIX = W_CONV + W_DIFF + W_MEM
CONV_GROUP_DIM = 128
CONV_GROUPS = W_CONV // CONV_GROUP_DIM
CONV_WIDTH = 3
DIFF_V_DIM = 128
DIFF_HALF_DIM = 64
DIFF_HEADS = W_DIFF // DIFF_V_DIM
MEM_HEAD_DIM = 128
MEM_HEADS = W_MEM // MEM_HEAD_DIM
Q_BLOCK = 128
EPS = 1e-6
SPLIT_SIZES = (W_CONV, W_CONV, W_CONV, W_CONV,
               W_DIFF, W_DIFF, W_DIFF, W_DIFF,
               W_MEM, W_MEM)
IN_COLS = sum(SPLIT_SIZES)

kernel_name = "hybrid_conv_diffattn_memxattn_encoder"


def rmsnorm(x, g):
    xf = x.astype(jnp.float32)
    y = xf * lax.rsqrt(jnp.mean(xf * xf, axis=-1, keepdims=True) + EPS)
    return (y * g.astype(jnp.float32)).astype(x.dtype)


def alibi_slopes(n_heads):
    return jnp.asarray(np.array([2.0 ** (-8.0 * (h + 1) / n_heads) for h in range(n_heads)], dtype=np.float32))


def short_gated_conv(a_x, a_b, a_c, w, b):
    u = a_c * a_x
    up = jnp.pad(u, ((0, 0), (1, 1), (0, 0)))
    z = up[:, :-2] * w[0] + up[:, 1:-1] * w[1] + up[:, 2:] * w[2] + b
    return a_b * z


def diff_attention(q, k, v, lam, slopes):
    b, s = q.shape[0], q.shape[1]
    nb = s // Q_BLOCK
    scale = DIFF_HALF_DIM ** -0.5
    kt = k.transpose(0, 2, 3, 1, 4)
    vt = v.transpose(0, 2, 1, 3)
    qb = q.reshape(b, nb, Q_BLOCK, DIFF_HEADS, 2, DIFF_HALF_DIM).transpose(1, 0, 3, 4, 2, 5)
    key_pos = jnp.arange(s, dtype=jnp.float32)

    def block(args):
        qi, i = args
        qpos = (i * Q_BLOCK + jnp.arange(Q_BLOCK)).astype(jnp.float32)
        bias = -slopes[:, None, None] * jnp.abs(qpos[:, None] - key_pos[None, :])
        sc = jnp.einsum('bhcqd,bhckd->bhcqk', qi, kt).astype(jnp.float32) * scale + bias[None, :, None]
        p = jax.nn.softmax(sc, axis=-1).astype(vt.dtype)
        o = jnp.einsum('bhcqk,bhkd->bhcqd', p, vt)
        return o[:, :, 0] - lam.astype(o.dtype) * o[:, :, 1]

    out = lax.map(block, (qb, jnp.arange(nb)))
    return out.transpose(1, 0, 3, 2, 4).reshape(b, s, DIFF_HEADS, DIFF_V_DIM)


def memory_attention(q, mk, mv):
    sc = jnp.einsum('bshd,bmhd->bhsm', q, mk).astype(jnp.float32) * (MEM_HEAD_DIM ** -0.5)
    p = jax.nn.softmax(sc, axis=-1).astype(mv.dtype)
    return jnp.einsum('bhsm,bmhd->bshd', p, mv)


def setup_inputs(seed: int = 0) -> dict:
    key = jax.random.key(seed)
    ks = jax.random.split(key, 18)
    nrm = jax.random.normal
    f32 = jnp.float32
    gain = lambda k, shape: 1.0 + 0.02 * nrm(k, shape, f32)
    return {
        "x": nrm(ks[0], (BATCH, SEQ, D_MODEL), f32),
        "mem": nrm(ks[1], (BATCH, N_MEM, D_MODEL), f32),
        "norm_g": gain(ks[2], (DEPTH, D_MODEL)),
        "w_in": nrm(ks[3], (DEPTH, D_MODEL, IN_COLS), f32) * D_MODEL ** -0.5,
        "conv_w": nrm(ks[4], (DEPTH, CONV_WIDTH, W_CONV), f32) * CONV_WIDTH ** -0.5,
        "conv_b": 0.02 * nrm(ks[5], (DEPTH, W_CONV), f32),
        "diff_q_norm_g": gain(ks[6], (DEPTH, DIFF_HALF_DIM)),
        "diff_k_norm_g": gain(ks[7], (DEPTH, DIFF_HALF_DIM)),
        "lambda_q1": 0.1 * nrm(ks[8], (DEPTH, DIFF_HALF_DIM), f32),
        "lambda_k1": 0.1 * nrm(ks[9], (DEPTH, DIFF_HALF_DIM), f32),
        "lambda_q2": 0.1 * nrm(ks[10], (DEPTH, DIFF_HALF_DIM), f32),
        "lambda_k2": 0.1 * nrm(ks[11], (DEPTH, DIFF_HALF_DIM), f32),
        "diff_head_norm_g": gain(ks[12], (DEPTH, DIFF_V_DIM)),
        "mem_norm_g": gain(ks[13], (DEPTH, D_MODEL)),
        "w_mem_kv": nrm(ks[14], (DEPTH, D_MODEL, 2 * W_MEM), f32) * D_MODEL ** -0.5,
        "mem_q_norm_g": gain(ks[15], (DEPTH, MEM_HEAD_DIM)),
        "mem_k_norm_g": gain(ks[16], (DEPTH, MEM_HEAD_DIM)),
        "w_out": nrm(ks[17], (DEPTH, W_MIX, D_MODEL), f32) * W_MIX ** -0.5,
    }


def reference(x, mem, norm_g, w_in, conv_w, conv_b, diff_q_norm_g, diff_k_norm_g,
              lambda_q1, lambda_k1, lambda_q2, lambda_k2, diff_head_norm_g,
              mem_norm_g, w_mem_kv, mem_q_norm_g, mem_k_norm_g, w_out):
    b, s = x.shape[0], x.shape[1]
    slopes = alibi_slopes(DIFF_HEADS)
    offsets = list(np.cumsum(SPLIT_SIZES)[:-1])
    for l in range(DEPTH):
        h = rmsnorm(x, norm_g[l])
        proj = h @ w_in[l]
        a_x, a_b, a_c, a_g, d_q, d_k, d_v, d_g, m_q, m_g = jnp.split(proj, offsets, axis=-1)

        y_a = short_gated_conv(a_x, a_b, a_c, conv_w[l], conv_b[l]) * jax.nn.silu(a_g)

        q = rmsnorm(d_q.reshape(b, s, DIFF_HEADS, 2, DIFF_HALF_DIM), diff_q_norm_g[l])
        k = rmsnorm(d_k.reshape(b, s, DIFF_HEADS, 2, DIFF_HALF_DIM), diff_k_norm_g[l])
        v = d_v.reshape(b, s, DIFF_HEADS, DIFF_V_DIM)
        lam_init = 0.8 - 0.6 * float(np.exp(-0.3 * l))
        lam = (jnp.exp(jnp.sum(lambda_q1[l].astype(jnp.float32) * lambda_k1[l].astype(jnp.float32)))
               - jnp.exp(jnp.sum(lambda_q2[l].astype(jnp.float32) * lambda_k2[l].astype(jnp.float32)))
               + lam_init)
        o_d = diff_attention(q, k, v, lam, slopes)
        o_d = rmsnorm(o_d, diff_head_norm_g[l]) * (1.0 - lam_init)
        y_d = o_d.reshape(b, s, W_DIFF) * jax.nn.silu(d_g)

        mn = rmsnorm(mem, mem_norm_g[l])
        mkv = mn @ w_mem_kv[l]
        mk = rmsnorm(mkv[..., :W_MEM].reshape(b, N_MEM, MEM_HEADS, MEM_HEAD_DIM), mem_k_norm_g[l])
        mv = mkv[..., W_MEM:].reshape(b, N_MEM, MEM_HEADS, MEM_HEAD_DIM)
        mq = rmsnorm(m_q.reshape(b, s, MEM_HEADS, MEM_HEAD_DIM), mem_q_norm_g[l])
        y_m = memory_attention(mq, mk, mv).reshape(b, s, W_MEM) * jax.nn.silu(m_g)

        y = jnp.concatenate([y_a, y_d, y_m], axis=-1) @ w_out[l]
        x = x + y
    return x
```

```python
import numpy as np
import ml_dtypes
import concourse.bass as bass
import concourse.mybir as mybir
from concourse.bass_utils import run_bass_kernel_spmd

F32 = mybir.dt.float32
BF16 = mybir.dt.bfloat16
AF = mybir.ActivationFunctionType
ALU = mybir.AluOpType
AX = mybir.AxisListType

D = 2048
KC = 16
NH = 6
NMH = 4
NMEM = 256
INC = 7168
EPS = 1e-6
NR = 4
NEG = -30000.0
ENG = ("sp", "act", "dve", "pool", "pe")
C_NG, C_MNG, C_W0, C_W1, C_W2, C_CB, C_GQ, C_GK, C_GH, C_GMQ, C_GMK, C_LAM = 0, 16, 32, 38, 44, 50, 56, 57, 58, 59, 60, 61
NCOLP = 65


class _I:
    def __getattr__(self, name):
        def f(*a, **k):
            return (name, a, k)
        return f


I = _I()


class Buf:
    def __init__(self, name, dsem=None):
        self.name = name
        self.w = None
        self.r = {}
        self.dsem = dsem
        self.dcnt = 0


class Sch:
    def __init__(self, nc):
        self.nc = nc
        self.q = {e: [] for e in ENG}
        self.cnt = {e: 0 for e in ENG}
        self.sem = {e: nc.alloc_semaphore("es_" + e) for e in ENG}
        self.seen = {e: {} for e in ENG}
        self.nsem = 5
        self.free = []
        self.dreg = {}

    def newsem(self, name):
        if self.free:
            return self.free.pop()
        self.nsem += 1
        sem = self.nc.alloc_semaphore("d%d" % self.nsem)
        self.dreg[id(sem)] = [sem, 0]
        return sem

    def relsem(self, sem):
        self.free.append(sem)

    def barrier(self):
        tks = [(self.sem[e], self.cnt[e]) for e in ENG if self.cnt[e] > 0]
        tks += [(sv[0], sv[1]) for sv in self.dreg.values() if sv[1] > 0]
        for e in ENG:
            for tk in tks:
                self._wait(e, tk)

    def _wait(self, e, tk):
        if tk is None:
            return
        sem, val = tk
        if e == "pe" and sem is self.sem["pe"]:
            return
        key = id(sem)
        if self.seen[e].get(key, 0) >= val:
            return
        self.seen[e][key] = val
        self.q[e].append(("w", sem, val))

    def _deps(self, e, reads, writes):
        for b in reads:
            self._wait(e, b.w)
        for b in writes:
            self._wait(e, b.w)
            for tk in b.r.values():
                self._wait(e, tk)

    def _commit(self, tk, reads, writes):
        for b in reads:
            b.r[id(tk[0])] = tk
        for b in writes:
            b.w = tk
            b.r = {}

    def op(self, e, fns, reads=(), writes=()):
        if isinstance(fns, tuple):
            fns = [fns]
        self._deps(e, reads, writes)
        self.cnt[e] += 1
        tk = (self.sem[e], self.cnt[e])
        for f in fns[:-1]:
            self.q[e].append(("i", f, None))
        self.q[e].append(("i", fns[-1], (self.sem[e], 1)))
        self._commit(tk, reads, writes)
        return tk

    def dma(self, e, fn, sb, reads=(), writes=()):
        self._deps(e, reads, writes)
        rec = self.dreg[id(sb.dsem)]
        rec[1] += 16
        tk = (sb.dsem, rec[1])
        self.q[e].append(("i", fn, (sb.dsem, 16)))
        self._commit(tk, reads, writes)
        return tk

    def emit(self, e, eng):
        for it in self.q[e]:
            if it[0] == "w":
                eng.wait_ge(it[1], it[2])
            else:
                name, a, k = it[1]
                ins = getattr(eng, name)(*a, **k)
                if it[2] is not None:
                    ins.then_inc(it[2][0], it[2][1])


DEBUG = False


def build_nc(T, DEPTH):
    NQT = T // 512
    TB = T // 128
    NTT = T // 128
    TW = 896
    TOFF = 384
    nc = bass.Bass("TRN2", target_bir_lowering=False)
    S = Sch(nc)

    def din(name, shape, dt=F32):
        return nc.dram_tensor(name, list(shape), dt, kind="ExternalInput").ap()

    x_in = din("x", [T, D])
    mem_in = din("mem", [NMEM, D])
    w_in = din("w_in", [DEPTH, D, INC])
    w_mkv = din("w_mem_kv", [DEPTH, D, 1024])
    w_out = din("w_out", [DEPTH, D, D])
    colp_in = din("colp", [DEPTH, 128, NCOLP])
    grow_in = din("grow", [DEPTH, 2, D])
    bcol_in = din("bcol", [128, NR * NH * NQT * TB])
    atab_in = din("atab", [128, NR * 3 * NH * 128], BF16)
    brow_in = din("brow", [128, 512], BF16)
    toep_in = din("toep", [128, TW])
    hsel_in = din("hsel", [128, 96])
    slp_in = din("slp", [128, NH])
    ident_in = din("ident", [128, 128], BF16)
    bd_in = din("bdones", [128, 128])
    out_d = nc.dram_tensor("out", [T, D], F32, kind="ExternalOutput").ap()

    xbuf = nc.dram_tensor("xbuf", [T, D], F32).ap()
    k_loc = nc.dram_tensor("k_loc", [NH * 128, T], BF16).ap()
    k_all = [nc.dram_tensor("k_all%d" % h, [NR * 128, T], BF16).ap() for h in range(NH)]
    v_loc = [nc.dram_tensor("v_loc%d" % h, [T, 128], BF16).ap() for h in range(NH)]
    v_all = [nc.dram_tensor("v_all%d" % h, [NR * T, 128], BF16).ap() for h in range(NH)]
    ue_loc = nc.dram_tensor("ue_loc", [2, 768], F32).ap()
    ue_all = nc.dram_tensor("ue_all", [2 * NR, 768], F32).ap()
    yT_scr = nc.dram_tensor("yT_scr", [16 * 128, T], BF16).ap()
    sgd_scr = nc.dram_tensor("sgd_scr", [NH * 128, T], F32).ap()
    if DEBUG:
        dbg_y = nc.dram_tensor("dbg_y", [16 * 128, T], BF16, kind="ExternalOutput").ap()

    B_xbuf = [Buf("xbuf%d" % i) for i in range(NTT)]
    B_kloc = [Buf("kloc%d" % i) for i in range(NH)]
    B_vloc = [Buf("vloc%d" % i) for i in range(NH)]
    B_kall = [Buf("kall%d" % i) for i in range(NH)]
    B_vall = [Buf("vall%d" % i) for i in range(NH)]
    B_ueloc = Buf("ueloc")
    B_ueall = Buf("ueall")
    B_yT = [Buf("yT%d" % i) for i in range(16)]
    B_sgd = [Buf("sgd%d" % i) for i in range(NH)]

    uid = {"n": 0}

    class Scope:
        def __init__(self):
            self.guards = []
            self.sems = []

        def close(self):
            S.barrier()
            for g in reversed(self.guards):
                g.__exit__(None, None, None)
            for sm in self.sems:
                S.relsem(sm)

    class Tile:
        def __init__(self, name, shape, dt, dma=False, scope=None):
            uid["n"] += 1
            nm = "%s_u%d" % (name, uid["n"])
            if scope is None:
                self.t = nc.alloc_sbuf_tensor(nm, list(shape), dt)
            else:
                g = nc.sbuf_tensor(nm, list(shape), dt)
                self.t = g.__enter__()
                scope.guards.append(g)
            sem = S.newsem(nm) if dma else None
            if sem is not None and scope is not None:
                scope.sems.append(sem)
            self.b = Buf(nm, sem)

    def pool(name, n, shape, dt, dma=False, scope=None):
        tiles = [Tile("%s%d" % (name, i), shape, dt, dma, scope) for i in range(n)]
        state = {"i": 0}

        def nxt():
            t = tiles[state["i"] % n]
            state["i"] += 1
            return t
        return nxt

    ident = Tile("ident", [128, 128], BF16, True)
    onesf = Tile("onesf", [128, 128], F32)
    bdf = Tile("bdf", [128, 128], F32, True)
    onesb = Tile("onesb", [128, 128], BF16)
    brow = Tile("brow", [128, 512], BF16, True)
    hsel = Tile("hsel", [128, 96], F32, True)
    slp = Tile("slp", [128, NH], F32, True)
    colp = [Tile("colp%d" % l, [128, NCOLP], F32, True) for l in range(DEPTH)]
    derv = [Tile("derv%d" % l, [128, 8], F32) for l in range(DEPTH)]
    for tl, src in ((ident, ident_in), (bdf, bd_in), (brow, brow_in), (hsel, hsel_in), (slp, slp_in)):
        S.dma("sp", I.dma_start(out=tl.t[:], in_=src), tl.b, writes=[tl.b])
    for l in range(DEPTH):
        S.dma("sp", I.dma_start(out=colp[l].t[:], in_=colp_in[l]), colp[l].b, writes=[colp[l].b])
    S.op("dve", I.memset(onesf.t[:], 1.0), writes=[onesf.b])
    S.op("dve", I.memset(onesb.t[:], 1.0), writes=[onesb.b])

    ps = nc.alloc_psum_tensor("ps", [128, 4096], F32)
    PB = [Buf("pb%d" % i) for i in range(8)]
    pst = {"c": 0}

    def nb():
        b = pst["c"] % 8
        pst["c"] += 1
        return b

    def nb2():
        if pst["c"] % 2:
            pst["c"] += 1
        b = pst["c"] % 8
        pst["c"] += 2
        return b

    def bank(b, n=1):
        return ps[:, b * 512:(b + n) * 512]

    tmp = pool("tmp", 10, [128, 512], F32)
    small = pool("small", 8, [128, 16], F32)
    ezg = Tile("ezg", [128, 24], F32)
    yedge = Tile("yedge", [128, 6, 2], BF16, True)
    mkT = Tile("mkT", [128, NMH, NMEM], BF16)
    mv = Tile("mv", [128, 2, 512], BF16)

    def rsqrt_chain(src_ap, src_bufs, mean_scale, width):
        t = tmp()
        S.op("dve", I.tensor_scalar(out=t.t[:, 0:width], in0=src_ap, scalar1=mean_scale, scalar2=EPS,
                                              op0=ALU.mult, op1=ALU.add), reads=src_bufs, writes=[t.b])
        S.op("act", I.activation(out=t.t[:, 0:width], in_=t.t[:, 0:width], func=AF.Ln),
             reads=[t.b], writes=[t.b])
        S.op("act", I.activation(out=t.t[:, 0:width], in_=t.t[:, 0:width], func=AF.Exp, scale=-0.5),
             reads=[t.b], writes=[t.b])
        return t

    def silu_from_psum(b, width=512):
        t = tmp()
        S.op("act", I.activation(out=t.t[:, 0:width], in_=bank(b)[:, 0:width], func=AF.Exp, scale=-1.0),
             reads=[PB[b]], writes=[t.b])
        S.op("dve", I.tensor_scalar(out=t.t[:, 0:width], in0=t.t[:, 0:width], scalar1=1.0, scalar2=None,
                                              op0=ALU.add), reads=[t.b], writes=[t.b])
        S.op("dve", I.reciprocal(out=t.t[:, 0:width], in_=t.t[:, 0:width]), reads=[t.b], writes=[t.b])
        S.op("dve", I.tensor_tensor(out=t.t[:, 0:width], in0=bank(b)[:, 0:width], in1=t.t[:, 0:width],
                                              op=ALU.mult), reads=[t.b, PB[b]], writes=[t.b])
        return t

    for l in range(DEPTH):
        lam_init = 0.8 - 0.6 * float(np.exp(-0.3 * l))
        cp = colp[l]
        dv = derv[l]
        last = (l == DEPTH - 1)
        x_src = x_in if l == 0 else xbuf
        x_dst = out_d if last else xbuf

        S.op("dve", I.tensor_scalar(out=dv.t[:, 0:1], in0=cp.t[:, C_GQ:C_GQ + 1], scalar1=0.125, scalar2=None,
                                              op0=ALU.mult), reads=[cp.b], writes=[dv.b])
        S.op("dve", I.tensor_scalar(out=dv.t[:, 2:3], in0=cp.t[:, C_GH:C_GH + 1], scalar1=float(1.0 - lam_init),
                                              scalar2=None, op0=ALU.mult), reads=[cp.b, dv.b], writes=[dv.b])
        S.op("dve", I.tensor_scalar(out=dv.t[:, 3:4], in0=cp.t[:, C_GMQ:C_GMQ + 1], scalar1=float(128 ** -0.5),
                                              scalar2=None, op0=ALU.mult), reads=[cp.b, dv.b], writes=[dv.b])
        lt = small()
        S.op("dve", I.tensor_tensor(out=lt.t[:, 0:1], in0=cp.t[:, C_LAM:C_LAM + 1], in1=cp.t[:, C_LAM + 1:C_LAM + 2],
                                              op=ALU.mult), reads=[cp.b], writes=[lt.b])
        S.op("dve", I.tensor_tensor(out=lt.t[:, 1:2], in0=cp.t[:, C_LAM + 2:C_LAM + 3], in1=cp.t[:, C_LAM + 3:C_LAM + 4],
                                              op=ALU.mult), reads=[cp.b, lt.b], writes=[lt.b])
        b = nb()
        S.op("pe", I.matmul(bank(b)[:, 0:2], onesf.t[:], lt.t[:, 0:2], start=True, stop=True),
             reads=[onesf.b, lt.b], writes=[PB[b]])
        S.op("act", I.activation(out=lt.t[:, 2:4], in_=bank(b)[:, 0:2], func=AF.Exp), reads=[PB[b], lt.b], writes=[lt.b])
        S.op("dve", I.tensor_tensor(out=lt.t[:, 4:5], in0=lt.t[:, 3:4], in1=lt.t[:, 2:3], op=ALU.subtract),
             reads=[lt.b], writes=[lt.b])
        S.op("dve", I.tensor_scalar(out=dv.t[:, 4:5], in0=lt.t[:, 4:5], scalar1=float(-lam_init), scalar2=None,
                                              op0=ALU.add), reads=[lt.b, dv.b], writes=[dv.b])

        sc_PA = Scope()
        qT = [Tile("qT%d" % h, [128, T], BF16, False, sc_PA) for h in range(NH)]
        yst = pool("yst", 2, [128, T], BF16, True, sc_PA)
        sc_NP = Scope()
        hT = Tile("hT_%d" % l, [128, KC, T], BF16, False, sc_NP)
        mnT = Tile("mnT_%d" % l, [128, KC, NMEM], BF16, False, sc_NP)
        sc_N = Scope()
        grow = [Tile("grow%d_%d" % (i, l), [128, D], F32, True, sc_N) for i in range(2)]
        hn_pool = pool("hn_%d" % l, 2, [128, D], BF16, False, sc_N)
        sq_t = Tile("sqj_%d" % l, [128, D], F32, False, sc_N)
        xt_pool = pool("xt", 2, [128, D], F32, True, sc_N)
        for i in range(2):
            S.dma("sp", I.dma_start(out=grow[i].t[:], in_=grow_in[l, i:i + 1, :].partition_broadcast(128)),
                  grow[i].b, writes=[grow[i].b])

        def norm_tile(src_ap, src_bufs, gtile, dstT, col0):
            xt = xt_pool()
            S.dma("sp", I.dma_start(out=xt.t[:], in_=src_ap), xt.b, reads=src_bufs, writes=[xt.b])
            S.op("act", I.activation(out=sq_t.t[:], in_=xt.t[:], func=AF.Square), reads=[xt.b], writes=[sq_t.b])
            st = small()
            S.op("dve", I.reduce_sum(out=st.t[:, 0:1], in_=sq_t.t[:], axis=AX.X), reads=[sq_t.b], writes=[st.b])
            S.op("dve", I.tensor_scalar(out=st.t[:, 1:2], in0=st.t[:, 0:1], scalar1=1.0 / D, scalar2=EPS,
                                                  op0=ALU.mult, op1=ALU.add), reads=[st.b], writes=[st.b])
            S.op("act", I.activation(out=st.t[:, 2:3], in_=st.t[:, 1:2], func=AF.Ln), reads=[st.b], writes=[st.b])
            S.op("act", I.activation(out=st.t[:, 3:4], in_=st.t[:, 2:3], func=AF.Exp, scale=-0.5), reads=[st.b], writes=[st.b])
            hn = hn_pool()
            S.op("dve", I.scalar_tensor_tensor(out=hn.t[:], in0=xt.t[:], scalar=st.t[:, 3:4], in1=gtile.t[:],
                                                         op0=ALU.mult, op1=ALU.mult), reads=[xt.b, st.b, gtile.b], writes=[hn.b])
            for half in range(2):
                b = nb()
                pv = bank(b).bitcast(BF16)
                S.op("pe", [I.transpose(out=pv[:, j * 128:(j + 1) * 128],
                                                                          in_=hn.t[:, (half * 8 + j) * 128:(half * 8 + j + 1) * 128],
                                                                          identity=ident.t[:]) for j in range(8)],
                     reads=[hn.b, ident.b], writes=[PB[b]])
                S.op("act", I.activation(
                    out=dstT.t[:, half * 8:(half + 1) * 8, col0:col0 + 128],
                    in_=pv.rearrange("p (j c) -> p j c", c=128), func=AF.Copy), reads=[PB[b]], writes=[dstT.b])

        for tt in range(NTT):
            norm_tile(x_src[tt * 128:(tt + 1) * 128, :], [B_xbuf[tt]] if l > 0 else [], grow[0], hT, tt * 128)
        for tt in range(2):
            norm_tile(mem_in[tt * 128:(tt + 1) * 128, :], [], grow[1], mnT, tt * 128)

        sc_N.close()
        sc_P = Scope()
        wpool = pool("wc_%d" % l, 6, [128, KC, 128], BF16, True, sc_P)

        def load_w(src2d, c0):
            w = wpool()
            S.dma("pool", I.dma_start(out=w.t[:], in_=src2d.rearrange("(kc p) c -> p kc c", p=128)[:, :, c0:c0 + 128]),
                  w.b, writes=[w.b])
            return w

        jobs = []
        for m in range(NMH):
            jobs.append(("mk", m, m))
        for m in range(NMH):
            jobs.append(("mv", m, NMH + m))
        for h in range(NH):
            jobs.append(("k", h, 30 + h))
        for h in range(NH):
            jobs.append(("v", h, 36 + h))
        for ci in range(6):
            jobs.append(("ax", ci, ci))
            jobs.append(("ac", ci, 12 + ci))
            jobs.append(("ab", ci, 6 + ci))
            jobs.append(("ag", ci, 18 + ci))
        for h in range(NH):
            jobs.append(("q", h, 24 + h))
        for h in range(NH):
            jobs.append(("dg", h, 42 + h))
        for m in range(NMH):
            jobs.append(("mq", m, 48 + m))
            jobs.append(("mg", m, 52 + m))
        wq = {}
        PF = 4

        def ensure_loaded(i):
            for j in range(0, min(i + PF + 1, len(jobs))):
                if j not in wq:
                    kind, a, c = jobs[j]
                    src = w_mkv[l] if kind in ("mk", "mv") else w_in[l]
                    wq[j] = load_w(src, c * 128)

        def proj_fm(w, rhsT, col0, width):
            b = nb()
            S.op("pe", [I.matmul(bank(b)[:, 0:width], w.t[:, kc, :], rhsT.t[:, kc, col0:col0 + width],
                                                       start=(kc == 0), stop=(kc == KC - 1)) for kc in range(KC)],
                 reads=[w.b, rhsT.b], writes=[PB[b]])
            return b

        def qk_norm(b, width, ones_t, mean_scale, gcol_ap, gbufs, out_ap, out_bufs):
            xs = tmp()
            sq = tmp()
            S.op("act", I.activation(out=xs.t[:, 0:width], in_=bank(b)[:, 0:width], func=AF.Copy), reads=[PB[b]], writes=[xs.b])
            S.op("act", I.activation(out=sq.t[:, 0:width], in_=bank(b)[:, 0:width], func=AF.Square), reads=[PB[b]], writes=[sq.b])
            b2 = nb()
            S.op("pe", I.matmul(bank(b2)[:, 0:width], ones_t.t[:], sq.t[:, 0:width], start=True, stop=True),
                 reads=[ones_t.b, sq.b], writes=[PB[b2]])
            rs = rsqrt_chain(bank(b2)[:, 0:width], [PB[b2]], mean_scale, width)
            S.op("dve", I.scalar_tensor_tensor(out=out_ap, in0=xs.t[:, 0:width], scalar=gcol_ap, in1=rs.t[:, 0:width],
                                                         op0=ALU.mult, op1=ALU.mult), reads=[xs.b, rs.b] + gbufs, writes=out_bufs)

        u_pool = pool("u_%d" % l, 1, [128, T + 2], F32, True, sc_P)
        for _ in range(1):
            ut = u_pool()
            S.op("dve", I.memset(ut.t[:], 0.0), writes=[ut.b])
        vst = Tile("vst_%d" % l, [128, NTT, 128], BF16, True, sc_P)
        kst_pool = pool("kst_%d" % l, 2, [128, T], BF16, True, sc_P)
        sgst_pool = pool("sgst_%d" % l, 1, [128, T], F32, True, sc_P)
        om_pool = pool("om_%d" % l, 1, [128, T], F32, False, sc_P)
        pm_pool = pool("pm_%d" % l, 2, [128, 1024], BF16, False, sc_P)
        mqn_pool = pool("mqn_%d" % l, 2, [128, 512], BF16, False, sc_P)
        u_cur = None
        om_cur = None
        ensure_loaded(0)
        for ji, (kind, a, c) in enumerate(jobs):
            if ji > 0:
                ensure_loaded(ji)
            w = wq[ji]
            if kind == "mk":
                b = proj_fm(w, mnT, 0, NMEM)
                qk_norm(b, NMEM, onesf, 1.0 / 128, cp.t[:, C_GMK:C_GMK + 1], [cp.b], mkT.t[:, a, :], [mkT.b])
            elif kind == "mv":
                for mb in range(2):
                    b = nb()
                    S.op("pe", [I.matmul(bank(b)[:, 0:128], mnT.t[:, kc, mb * 128:(mb + 1) * 128], w.t[:, kc, :],
                                                                       start=(kc == 0), stop=(kc == KC - 1)) for kc in range(KC)],
                         reads=[w.b, mnT.b], writes=[PB[b]])
                    S.op("act", I.activation(out=mv.t[:, mb, a * 128:(a + 1) * 128], in_=bank(b)[:, 0:128], func=AF.Copy),
                         reads=[PB[b]], writes=[mv.b])
            elif kind == "k":
                kst = kst_pool()
                for qt in range(NQT):
                    b = proj_fm(w, hT, qt * 512, 512)
                    qk_norm(b, 512, bdf, 1.0 / 64, cp.t[:, C_GK:C_GK + 1], [cp.b], kst.t[:, qt * 512:(qt + 1) * 512], [kst.b])
                S.dma("sp", I.dma_start(out=k_loc[a * 128:(a + 1) * 128, :], in_=kst.t[:]), kst.b,
                      reads=[kst.b], writes=[B_kloc[a]])
                S.op("pool", I.collective_compute("AllGather", ALU.bypass, replica_groups=[[0, 1, 2, 3], [4, 5, 6, 7]],
                                                  ins=[k_loc[a * 128:(a + 1) * 128, :].opt()], outs=[k_all[a].opt()]),
                     reads=[B_kloc[a]], writes=[B_kall[a]])
            elif kind == "v":
                for t4 in range(NTT // 4):
                    b = nb()
                    fns = []
                    for j in range(4):
                        tt = t4 * 4 + j
                        for kc in range(KC):
                            fns.append(I.matmul(bank(b)[:, j * 128:(j + 1) * 128],
                                                                                  hT.t[:, kc, tt * 128:(tt + 1) * 128], w.t[:, kc, :],
                                                                                  start=(kc == 0), stop=(kc == KC - 1)))
                    S.op("pe", fns, reads=[w.b, hT.b], writes=[PB[b]])
                    S.op("act", I.activation(out=vst.t[:, t4 * 4:(t4 + 1) * 4, :],
                                                                   in_=bank(b).rearrange("p (j c) -> p j c", c=128), func=AF.Copy),
                         reads=[PB[b]], writes=[vst.b])
                S.dma("sp", I.dma_start(out=v_loc[a].rearrange("(tt p) c -> p tt c", p=128), in_=vst.t[:]), vst.b,
                      reads=[vst.b], writes=[B_vloc[a]])
                S.op("pool", I.collective_compute("AllGather", ALU.bypass, replica_groups=[[0, 1, 2, 3], [4, 5, 6, 7]],
                                                  ins=[v_loc[a].opt()], outs=[v_all[a].opt()]),
                     reads=[B_vloc[a]], writes=[B_vall[a]])
            elif kind == "ax":
                u_cur = u_pool()
                for qt in range(NQT):
                    b = proj_fm(w, hT, qt * 512, 512)
                    S.op("act", I.activation(out=u_cur.t[:, 1 + qt * 512:1 + (qt + 1) * 512], in_=bank(b), func=AF.Copy),
                         reads=[PB[b]], writes=[u_cur.b])
            elif kind == "ac":
                u = u_cur
                for qt in range(NQT):
                    b = proj_fm(w, hT, qt * 512, 512)
                    S.op("dve", I.tensor_tensor(out=u.t[:, 1 + qt * 512:1 + (qt + 1) * 512],
                                                                               in0=bank(b), in1=u.t[:, 1 + qt * 512:1 + (qt + 1) * 512], op=ALU.mult),
                         reads=[PB[b], u_cur.b], writes=[u_cur.b])
                S.dma("sp", I.dma_start(out=ue_loc[0:1, a * 128:(a + 1) * 128].rearrange("o p -> p o"), in_=u.t[:, 1:2], allow_slow_non_contiguous=True),
                      u_cur.b, reads=[u_cur.b], writes=[B_ueloc])
                S.dma("sp", I.dma_start(out=ue_loc[1:2, a * 128:(a + 1) * 128].rearrange("o p -> p o"), in_=u.t[:, T:T + 1], allow_slow_non_contiguous=True),
                      u_cur.b, reads=[u_cur.b], writes=[B_ueloc])
            elif kind == "ab":
                ab_w = w
            elif kind == "ag":
                y = yst()
                for qt in range(NQT):
                    bb = proj_fm(ab_w, hT, qt * 512, 512)
                    bg = proj_fm(w, hT, qt * 512, 512)
                    sg = silu_from_psum(bg)
                    S.op("dve", I.tensor_tensor(out=sg.t[:], in0=bank(bb), in1=sg.t[:], op=ALU.mult),
                         reads=[PB[bb], sg.b], writes=[sg.b])
                    z = tmp()
                    u = u_cur
                    o0 = qt * 512
                    S.op("dve", I.tensor_scalar(out=z.t[:], in0=u.t[:, o0:o0 + 512], scalar1=cp.t[:, C_W0 + a:C_W0 + a + 1],
                                                                                scalar2=cp.t[:, C_CB + a:C_CB + a + 1], op0=ALU.mult, op1=ALU.add),
                         reads=[u.b, cp.b], writes=[z.b])
                    S.op("dve", I.scalar_tensor_tensor(out=z.t[:], in0=u.t[:, o0 + 1:o0 + 513], scalar=cp.t[:, C_W1 + a:C_W1 + a + 1],
                                                                                       in1=z.t[:], op0=ALU.mult, op1=ALU.add),
                         reads=[u.b, cp.b, z.b], writes=[z.b])
                    S.op("dve", I.scalar_tensor_tensor(out=z.t[:], in0=u.t[:, o0 + 2:o0 + 514], scalar=cp.t[:, C_W2 + a:C_W2 + a + 1],
                                                                                       in1=z.t[:], op0=ALU.mult, op1=ALU.add),
                         reads=[u.b, cp.b, z.b], writes=[z.b])
                    S.op("dve", I.tensor_tensor(out=y.t[:, o0:o0 + 512], in0=z.t[:], in1=sg.t[:], op=ALU.mult),
                         reads=[z.b, sg.b], writes=[y.b])
                    if qt == 0:
                        S.op("dve", I.tensor_copy(out=ezg.t[:, 2 * a:2 * a + 1], in_=z.t[:, 0:1]), reads=[z.b, ezg.b], writes=[ezg.b])
                        S.op("dve", I.tensor_copy(out=ezg.t[:, 12 + 2 * a:12 + 2 * a + 1], in_=sg.t[:, 0:1]), reads=[sg.b, ezg.b], writes=[ezg.b])
                    if qt == NQT - 1:
                        S.op("dve", I.tensor_copy(out=ezg.t[:, 2 * a + 1:2 * a + 2], in_=z.t[:, 511:512]), reads=[z.b, ezg.b], writes=[ezg.b])
                        S.op("dve", I.tensor_copy(out=ezg.t[:, 12 + 2 * a + 1:12 + 2 * a + 2], in_=sg.t[:, 511:512]), reads=[sg.b, ezg.b], writes=[ezg.b])
                S.dma("sp", I.dma_start(out=yT_scr[a * 128:(a + 1) * 128, :], in_=y.t[:]), y.b, reads=[y.b], writes=[B_yT[a]])
                if a == 5:
                    S.op("pool", I.collective_compute("AllGather", ALU.bypass, replica_groups=[[0, 1, 2, 3], [4, 5, 6, 7]],
                                                                ins=[ue_loc.opt()], outs=[ue_all.opt()]),
                         reads=[B_ueloc], writes=[B_ueall])
            elif kind == "q":
                for qt in range(NQT):
                    b = proj_fm(w, hT, qt * 512, 512)
                    qk_norm(b, 512, bdf, 1.0 / 64, dv.t[:, 0:1], [dv.b], qT[a].t[:, qt * 512:(qt + 1) * 512], [qT[a].b])
            elif kind == "dg":
                sgs = sgst_pool()
                for qt in range(NQT):
                    b = proj_fm(w, hT, qt * 512, 512)
                    sg = silu_from_psum(b)
                    S.op("act", I.activation(out=sgs.t[:, qt * 512:(qt + 1) * 512], in_=sg.t[:], func=AF.Copy),
                         reads=[sg.b], writes=[sgs.b])
                S.dma("sp", I.dma_start(out=sgd_scr[a * 128:(a + 1) * 128, :], in_=sgs.t[:]), sgs.b,
                      reads=[sgs.b], writes=[B_sgd[a]])
            elif kind == "mq":
                om_cur = om_pool()
                om = om_cur
                for qt in range(NQT):
                    b = proj_fm(w, hT, qt * 512, 512)
                    mqn = mqn_pool()
                    qk_norm(b, 512, onesf, 1.0 / 128, dv.t[:, 3:4], [dv.b], mqn.t[:], [mqn.b])
                    b2 = nb2()
                    for blk in range(2):
                        S.op("pe", I.matmul(bank(b2 + blk), mkT.t[:, a, blk * 128:(blk + 1) * 128], mqn.t[:],
                                                                               start=True, stop=True),
                             reads=[mkT.b, mqn.b], writes=[PB[b2 + blk]])
                    pm = pm_pool()
                    S.op("act", I.activation(out=pm.t[:], in_=bank(b2, 2), func=AF.Exp),
                         reads=[PB[b2], PB[b2 + 1]], writes=[pm.b])
                    bo = nb()
                    bd_ = nb()
                    S.op("pe", [I.matmul(bank(bo), mv.t[:, blk, a * 128:(a + 1) * 128], pm.t[:, blk * 512:(blk + 1) * 512],
                                                                          start=(blk == 0), stop=(blk == 1)) for blk in range(2)],
                         reads=[mv.b, pm.b], writes=[PB[bo]])
                    S.op("pe", [I.matmul(bank(bd_), onesb.t[:], pm.t[:, blk * 512:(blk + 1) * 512],
                                                                            start=(blk == 0), stop=(blk == 1)) for blk in range(2)],
                         reads=[onesb.b, pm.b], writes=[PB[bd_]])
                    rd = tmp()
                    S.op("dve", I.reciprocal(out=rd.t[:], in_=bank(bd_)), reads=[PB[bd_]], writes=[rd.b])
                    S.op("dve", I.tensor_tensor(out=om.t[:, qt * 512:(qt + 1) * 512], in0=bank(bo), in1=rd.t[:], op=ALU.mult),
                         reads=[PB[bo], rd.b], writes=[om_cur.b])
            elif kind == "mg":
                y = yst()
                om = om_cur
                for qt in range(NQT):
                    b = proj_fm(w, hT, qt * 512, 512)
                    sg = silu_from_psum(b)
                    S.op("dve", I.tensor_tensor(out=y.t[:, qt * 512:(qt + 1) * 512], in0=om.t[:, qt * 512:(qt + 1) * 512],
                                                                                        in1=sg.t[:], op=ALU.mult),
                         reads=[sg.b, om_cur.b], writes=[y.b])
                S.dma("sp", I.dma_start(out=yT_scr[(12 + a) * 128:(13 + a) * 128, :], in_=y.t[:]), y.b, reads=[y.b], writes=[B_yT[12 + a]])

        ue = Tile("ue_%d" % l, [128, 8, 6], F32, True, sc_P)
        S.dma("sp", I.dma_start(out=ue.t[:], in_=ue_all.rearrange("r (c p) -> p r c", p=128), allow_slow_non_contiguous=True),
              ue.b, reads=[B_ueall], writes=[ue.b])
        hl = small()
        ut = Tile("uet_%d" % l, [128, 6, 8], F32, False, sc_P)
        for side in range(2):
            S.op("dve", I.tensor_tensor(out=ut.t[:], in0=ue.t[:].rearrange("p r c -> p c r"), in1=hsel.t[:, side * 48:(side + 1) * 48].rearrange("p (c r) -> p c r", r=8),
                                                            op=ALU.mult), reads=[ue.b, hsel.b], writes=[ut.b])
            S.op("dve", I.reduce_sum(out=hl.t[:, side * 6:(side + 1) * 6], in_=ut.t[:], axis=AX.X), reads=[ut.b, hl.b], writes=[hl.b])
            wc = C_W0 if side == 0 else C_W2
            S.op("dve", I.tensor_tensor(out=hl.t[:, side * 6:(side + 1) * 6], in0=hl.t[:, side * 6:(side + 1) * 6],
                                                                    in1=cp.t[:, wc:wc + 6], op=ALU.mult), reads=[hl.b, cp.b], writes=[hl.b])
            ezv = ezg.t[:, 0:12].rearrange("p (c s) -> p c s", s=2)
            egv = ezg.t[:, 12:24].rearrange("p (c s) -> p c s", s=2)
            S.op("dve", I.tensor_tensor(out=hl.t[:, side * 6:(side + 1) * 6], in0=hl.t[:, side * 6:(side + 1) * 6],
                                                                      in1=ezv[:, :, side], op=ALU.add), reads=[hl.b, ezg.b], writes=[hl.b])
            S.op("dve", I.tensor_tensor(out=yedge.t[:, :, side], in0=hl.t[:, side * 6:(side + 1) * 6],
                                                                      in1=egv[:, :, side], op=ALU.mult), reads=[hl.b, ezg.b], writes=[yedge.b])
        yv = yT_scr.rearrange("(c p) t -> p c t", p=128)
        S.dma("sp", I.dma_start(out=yv[:, 0:6, 0:1], in_=yedge.t[:, :, 0:1], allow_slow_non_contiguous=True), yedge.b, reads=[yedge.b], writes=B_yT[0:6])
        S.dma("sp", I.dma_start(out=yv[:, 0:6, T - 1:T], in_=yedge.t[:, :, 1:2], allow_slow_non_contiguous=True), yedge.b, reads=[yedge.b], writes=B_yT[0:6])

        sc_P.close()
        sc_NP.close()
        sc_A = Scope()
        NSRC = NR + 1
        bcol = Tile("bcol", [128, NR * NH * NQT * TB], F32, True, sc_A)
        atab = Tile("atab", [128, NR * 3 * NH * 128], BF16, True, sc_A)
        toep = Tile("toep", [128, TW], F32, True, sc_A)
        for tl, src in ((bcol, bcol_in), (atab, atab_in), (toep, toep_in)):
            S.dma("sp", I.dma_start(out=tl.t[:], in_=src), tl.b, writes=[tl.b])
        kb_pool = pool("kb_%d" % l, 2 * NSRC - 2, [128, T], BF16, True, sc_A)
        vb_pool = pool("vb_%d" % l, 2 * NSRC - 2, [128, TB, 128], BF16, True, sc_A)
        sgd_pool = pool("sgdp_%d" % l, 2, [128, T], F32, True, sc_A)
        p_pool = pool("pt_%d" % l, 4, [128, 1024], BF16, False, sc_A)
        ssb_pool = pool("ssb_%d" % l, 2, [128, 1024], F32, False, sc_A)
        dacc_pool = pool("dacc_%d" % l, 2, [128, 1024], F32, False, sc_A)
        pending = []
        O1, O2, D1, D2 = 4, 5, 6, 7

        def load_src(h, s):
            kt = kb_pool()
            vt = vb_pool()
            if s < NR:
                ksrc = k_all[h][s * 128:(s + 1) * 128, :]
                vsrc = v_all[h][s * T:(s + 1) * T, :].rearrange("(kb p) d -> p kb d", p=128)
                rb_k, rb_v = [B_kall[h]], [B_vall[h]]
            else:
                ksrc = k_loc[h * 128:(h + 1) * 128, :]
                vsrc = v_loc[h].rearrange("(kb p) d -> p kb d", p=128)
                rb_k, rb_v = [B_kloc[h]], [B_vloc[h]]
            S.dma("sp", I.dma_start(out=kt.t[:], in_=ksrc), kt.b, reads=rb_k, writes=[kt.b])
            S.dma("sp", I.dma_start(out=vt.t[:], in_=vsrc), vt.b, reads=rb_v, writes=[vt.b])
            return (kt, vt)

        def load_sgd(h):
            sg = sgd_pool()
            S.dma("sp", I.dma_start(out=sg.t[:], in_=sgd_scr[h * 128:(h + 1) * 128, :]), sg.b, reads=[B_sgd[h]], writes=[sg.b])
            return sg

        NEARLY = NSRC - 2
        cur_srcs = [load_src(0, s) for s in range(NSRC)]
        cur_sgd = load_sgd(0)
        pst["c"] = 0
        for h in range(NH):
            srcs, sgd = cur_srcs, cur_sgd
            if h + 1 < NH:
                nxt_srcs = [load_src(h + 1, s) for s in range(NEARLY)]
                nxt_sgd = load_sgd(h + 1)
            y = yst()
            for qt in range(NQT):
                q0 = qt * 512
                blocks = [(s, kb) for s in range(NR) for kb in range(TB)] + [(NR, kb) for kb in range(4 * qt, 4 * qt + 4)]
                nblk = len(blocks)
                dacc = dacc_pool()
                pe_d = [bi for bi in range(nblk) if bi % 6 == 5 and bi >= 11]
                pe_d_first = pe_d[0] if pe_d else -1

                def emit_S(bi):
                    s, kb = blocks[bi]
                    kt, vt = srcs[s]
                    b2 = (bi % 2) * 2
                    pt = p_pool()
                    if s < NR:
                        cls = 0 if kb < 4 * qt else (1 if kb < 4 * qt + 4 else 2)
                        acol = ((s * 3 + cls) * NH + h) * 128
                        bidx = ((s * NH + h) * NQT + qt) * TB + kb
                        fns = []
                        for hf in range(2):
                            fns.append(I.matmul(bank(b2 + hf), kt.t[hf * 64:(hf + 1) * 64, kb * 128:(kb + 1) * 128],
                                                qT[h].t[hf * 64:(hf + 1) * 64, q0:q0 + 512], start=True, stop=False))
                        for hf in range(2):
                            fns.append(I.matmul(bank(b2 + hf), atab.t[64 * hf:64 * hf + 64, acol:acol + 128], brow.t[64 * hf:64 * hf + 64, :],
                                                start=False, stop=True))
                        S.op("pe", fns, reads=[kt.b, qT[h].b, atab.b, brow.b], writes=[PB[b2], PB[b2 + 1]])
                        S.op("act", I.activation(out=pt.t[:], in_=bank(b2, 2), func=AF.Exp, bias=bcol.t[:, bidx:bidx + 1]),
                             reads=[PB[b2], PB[b2 + 1], bcol.b], writes=[pt.b])
                    else:
                        fns = []
                        for hf in range(2):
                            fns.append(I.matmul(bank(b2 + hf), kt.t[hf * 64:(hf + 1) * 64, kb * 128:(kb + 1) * 128],
                                                qT[h].t[hf * 64:(hf + 1) * 64, q0:q0 + 512], start=True, stop=True))
                        S.op("pe", fns, reads=[kt.b, qT[h].b], writes=[PB[b2], PB[b2 + 1]])
                        ssb = ssb_pool()
                        m0 = TOFF - 128 * (kb - 4 * qt)
                        for hf in range(2):
                            S.op("dve", I.scalar_tensor_tensor(
                                out=ssb.t[:, hf * 512:(hf + 1) * 512], in0=toep.t[:, m0:m0 + 512], scalar=slp.t[:, h:h + 1], in1=bank(b2 + hf),
                                op0=ALU.mult, op1=ALU.add), reads=[toep.b, slp.b, PB[b2 + hf], ssb.b], writes=[ssb.b])
                        S.op("act", I.activation(out=pt.t[:], in_=ssb.t[:], func=AF.Exp), reads=[ssb.b], writes=[pt.b])
                    return pt

                def emit_PV(bi, pt):
                    s, kb = blocks[bi]
                    kt, vt = srcs[s]
                    first = (bi == 0)
                    lastb = (bi == nblk - 1)
                    fns = [I.matmul(bank(ob), vt.t[:, kb, :], pt.t[:, hf * 512:(hf + 1) * 512], start=first, stop=lastb)
                           for hf, ob in enumerate((O1, O2))]
                    S.op("pe", fns, reads=[vt.b, pt.b], writes=[PB[O1], PB[O2]])
                    if bi in pe_d:
                        fns = [I.matmul(bank(db), onesb.t[:], pt.t[:, hf * 512:(hf + 1) * 512], start=(bi == pe_d_first), stop=False)
                               for hf, db in enumerate((D1, D2))]
                        S.op("pe", fns, reads=[onesb.b, pt.b], writes=[PB[D1], PB[D2]])
                    elif first:
                        S.op("dve", I.tensor_copy(out=dacc.t[:], in_=pt.t[:]), reads=[pt.b], writes=[dacc.b])
                    else:
                        S.op("dve", I.tensor_tensor(out=dacc.t[:], in0=dacc.t[:], in1=pt.t[:], op=ALU.add), reads=[pt.b, dacc.b], writes=[dacc.b])

                pts = {}
                for bi in range(nblk):
                    pts[bi] = emit_S(bi)
                    if bi >= 1:
                        emit_PV(bi - 1, pts.pop(bi - 1))
                    if bi == 2 and pending:
                        pending.pop()()
                emit_PV(nblk - 1, pts.pop(nblk - 1))
                o1s = tmp()
                o2s = tmp()
                S.op("dve", I.tensor_copy(out=o1s.t[:], in_=bank(O1)), reads=[PB[O1]], writes=[o1s.b])
                S.op("dve", I.tensor_copy(out=o2s.t[:], in_=bank(O2)), reads=[PB[O2]], writes=[o2s.b])

                def epilogue_b(o1s=o1s, o2s=o2s, dacc=dacc, y=y, q0=q0, sgd=sgd, has_pe_d=bool(pe_d)):
                    for hf, db in enumerate((D1, D2)):
                        S.op("pe", I.matmul(bank(db), onesf.t[:], dacc.t[:, hf * 512:(hf + 1) * 512], start=(not has_pe_d), stop=True),
                             reads=[onesf.b, dacc.b], writes=[PB[db]])
                    r1 = tmp()
                    r2 = tmp()
                    for rr_, db in ((r1, D1), (r2, D2)):
                        S.op("act", I.activation(out=rr_.t[:], in_=bank(db), func=AF.Ln), reads=[PB[db]], writes=[rr_.b])
                        S.op("act", I.activation(out=rr_.t[:], in_=rr_.t[:], func=AF.Exp, scale=-1.0), reads=[rr_.b], writes=[rr_.b])
                    S.op("dve", I.tensor_tensor(out=o1s.t[:], in0=o1s.t[:], in1=r1.t[:], op=ALU.mult), reads=[o1s.b, r1.b], writes=[o1s.b])
                    S.op("dve", I.tensor_tensor(out=r2.t[:], in0=o2s.t[:], in1=r2.t[:], op=ALU.mult), reads=[o2s.b, r2.b], writes=[r2.b])
                    S.op("dve", I.scalar_tensor_tensor(out=o1s.t[:], in0=r2.t[:], scalar=dv.t[:, 4:5], in1=o1s.t[:], op0=ALU.mult, op1=ALU.add),
                         reads=[r2.b, o1s.b, dv.b], writes=[o1s.b])
                    sq = tmp()
                    S.op("act", I.activation(out=sq.t[:], in_=o1s.t[:], func=AF.Square), reads=[o1s.b], writes=[sq.b])
                    S.op("pe", I.matmul(bank(D1), onesf.t[:], sq.t[:], start=True, stop=True), reads=[onesf.b, sq.b], writes=[PB[D1]])
                    rs = rsqrt_chain(bank(D1), [PB[D1]], 1.0 / 128, 512)
                    S.op("dve", I.scalar_tensor_tensor(out=o1s.t[:], in0=o1s.t[:], scalar=dv.t[:, 2:3], in1=rs.t[:], op0=ALU.mult, op1=ALU.mult),
                         reads=[o1s.b, rs.b, dv.b], writes=[o1s.b])
                    S.op("dve", I.tensor_tensor(out=y.t[:, q0:q0 + 512], in0=o1s.t[:], in1=sgd.t[:, q0:q0 + 512], op=ALU.mult),
                         reads=[o1s.b, sgd.b], writes=[y.b])

                if qt == NQT - 1:
                    epilogue_b()
                else:
                    pending.append(epilogue_b)
            S.dma("sp", I.dma_start(out=yT_scr[(6 + h) * 128:(7 + h) * 128, :], in_=y.t[:]), y.b, reads=[y.b], writes=[B_yT[6 + h]])
            if h + 1 < NH:
                nxt_srcs += [load_src(h + 1, s) for s in range(NEARLY, NSRC)]
                cur_srcs, cur_sgd = nxt_srcs, nxt_sgd

        sc_A.close()
        sc_PA.close()
        if DEBUG and l == 0:
            S.dma("sp", I.dma_start(out=dbg_y, in_=yT_scr), ident.b, reads=B_yT, writes=[])
            S.barrier()
        sc_O = Scope()
        wo = [Tile("wo%d_%d" % (cg, l), [128, KC, 512], BF16, True, sc_O) for cg in range(4)]
        for cg in range(4):
            S.dma("pool", I.dma_start(out=wo[cg].t[:], in_=w_out[l].rearrange("(kc p) c -> p kc c", p=128)[:, :, cg * 512:(cg + 1) * 512]),
                  wo[cg].b, writes=[wo[cg].b])
        yg_pool = pool("yg_%d" % l, 2, [128, KC, 512], BF16, True, sc_O)
        xo_pool = pool("xo_%d" % l, 2, [128, D], F32, True, sc_O)
        yvv = yT_scr.rearrange("(c p) t -> p c t", p=128)
        pst["c"] = 0
        for g in range(NQT):
            yg = yg_pool()
            S.dma("sp", I.dma_start(out=yg.t[:], in_=yvv[:, :, g * 512:(g + 1) * 512]), yg.b, reads=B_yT, writes=[yg.b])
            for t4 in range(4):
                tt = g * 4 + t4
                xo = xo_pool()
                S.dma("sp", I.dma_start(out=xo.t[:], in_=x_src[tt * 128:(tt + 1) * 128, :]), xo.b,
                      reads=[B_xbuf[tt]] if l > 0 else [], writes=[xo.b])
                for cg in range(4):
                    b = nb()
                    S.op("pe", [I.matmul(bank(b), yg.t[:, kc, t4 * 128:(t4 + 1) * 128], wo[cg].t[:, kc, :],
                                                                                    start=(kc == 0), stop=(kc == KC - 1)) for kc in range(KC)],
                         reads=[yg.b, wo[cg].b], writes=[PB[b]])
                    S.op("dve", I.tensor_tensor(out=xo.t[:, cg * 512:(cg + 1) * 512], in0=bank(b), in1=xo.t[:, cg * 512:(cg + 1) * 512],
                                                                            op=ALU.add), reads=[PB[b], xo.b], writes=[xo.b])
                S.dma("sp", I.dma_start(out=x_dst[tt * 128:(tt + 1) * 128, :], in_=xo.t[:]), xo.b,
                      reads=[xo.b], writes=[B_xbuf[tt]])

        sc_O.close()

    S.barrier()

    with nc.Block() as block:
        @block.sync
        def _(e):
            S.emit("sp", e)

        @block.scalar
        def _(e):
            S.emit("act", e)

        @block.vector
        def _(e):
            S.emit("dve", e)

        @block.gpsimd
        def _(e):
            S.emit("pool", e)

        @block.tensor
        def _(e):
            S.emit("pe", e)
    return nc


def _bf16_parts(v, n=3):
    parts = []
    r = np.float64(v)
    for _ in range(n):
        p = np.float32(r).astype(ml_dtypes.bfloat16)
        parts.append(p)
        r = r - np.float64(p.astype(np.float32))
    return parts


def host_consts(T, r):
    NQT = T // 512
    TB = T // 128
    slopes = np.array([2.0 ** (-8.0 * (h + 1) / NH) for h in range(NH)], dtype=np.float32)
    jj = np.arange(128, dtype=np.float64)
    bcol = np.zeros((128, NR, NH, NQT, TB), np.float32)
    atab = np.zeros((128, NR, 3, NH, 128), dtype=ml_dtypes.bfloat16)
    for s in range(NR):
        for h in range(NH):
            sl = np.float64(slopes[h])
            for qt in range(NQT):
                i0 = r * T + 512 * qt
                for kb in range(TB):
                    j = s * T + 128 * kb + jj
                    if s == r and 4 * qt <= kb < 4 * qt + 4:
                        bcol[:, s, h, qt, kb] = NEG
                        continue
                    sig = 1.0 if (s < r or (s == r and kb < 4 * qt)) else -1.0
                    bcol[:, s, h, qt, kb] = (sig * sl * (j - i0)).astype(np.float32)
            for cls in range(3):
                if s == r:
                    sig = 1.0 if cls == 0 else -1.0
                else:
                    sig = 1.0 if s < r else -1.0
                parts = _bf16_parts(-sig * sl)
                for a in range(3):
                    for b in range(2):
                        atab[a * 2 + b, s, cls, h, :] = parts[a]
                        atab[64 + a * 2 + b, s, cls, h, :] = parts[a]
    ii = np.arange(512)
    brow = np.zeros((128, 512), dtype=ml_dtypes.bfloat16)
    hi = (ii - (ii % 2)).astype(np.float32)
    lo = (ii % 2).astype(np.float32)
    for a in range(3):
        for o in (0, 64):
            brow[o + a * 2 + 0] = hi.astype(ml_dtypes.bfloat16)
            brow[o + a * 2 + 1] = lo.astype(ml_dtypes.bfloat16)
    TW = 896
    TOFF = 384
    m = np.arange(TW, dtype=np.float64)[None, :]
    toep = (-np.abs(m - TOFF - jj[:, None])).astype(np.float32)
    hsel = np.zeros((128, 2, 6, 8), np.float32)
    if r > 0:
        hsel[:, 0, :, (r - 1) * 2 + 1] = 1.0
    if r < NR - 1:
        hsel[:, 1, :, (r + 1) * 2 + 0] = 1.0
    slp = np.tile(slopes[None, :], (128, 1)).astype(np.float32)
    return dict(bcol=bcol.reshape(128, -1), atab=atab.reshape(128, -1), brow=brow, toep=toep,
                hsel=hsel.reshape(128, 96), slp=slp)


def pack_params(DEPTH, norm_g, mem_norm_g, conv_w, conv_b, diff_q_norm_g, diff_k_norm_g, lambda_q1, lambda_k1,
                lambda_q2, lambda_k2, diff_head_norm_g, mem_q_norm_g, mem_k_norm_g):
    colp = np.zeros((DEPTH, 128, NCOLP), np.float32)
    for l in range(DEPTH):
        colp[l, :, C_NG:C_NG + 16] = norm_g[l].reshape(16, 128).T
        colp[l, :, C_MNG:C_MNG + 16] = mem_norm_g[l].reshape(16, 128).T
        for k, c0 in ((0, C_W0), (1, C_W1), (2, C_W2)):
            colp[l, :, c0:c0 + 6] = conv_w[l, k].reshape(6, 128).T
        colp[l, :, C_CB:C_CB + 6] = conv_b[l].reshape(6, 128).T
        colp[l, :, C_GQ] = np.tile(diff_q_norm_g[l], 2)
        colp[l, :, C_GK] = np.tile(diff_k_norm_g[l], 2)
        colp[l, :, C_GH] = diff_head_norm_g[l]
        colp[l, :, C_GMQ] = mem_q_norm_g[l]
        colp[l, :, C_GMK] = mem_k_norm_g[l]
        colp[l, 0:64, C_LAM + 0] = lambda_q1[l]
        colp[l, 0:64, C_LAM + 1] = lambda_k1[l]
        colp[l, 0:64, C_LAM + 2] = lambda_q2[l]
        colp[l, 0:64, C_LAM + 3] = lambda_k2[l]
    grow = np.stack([norm_g, mem_norm_g], axis=1).astype(np.float32)
    return colp, grow


_NC_CACHE = {}


def kernel(x, mem, norm_g, w_in, conv_w, conv_b, diff_q_norm_g, diff_k_norm_g,
           lambda_q1, lambda_k1, lambda_q2, lambda_k2, diff_head_norm_g,
           mem_norm_g, w_mem_kv, mem_q_norm_g, mem_k_norm_g, w_out):
    x = np.asarray(x, np.float32)
    mem = np.asarray(mem, np.float32)
    Bb, Sq, _ = x.shape
    DEPTH = int(np.asarray(w_in).shape[0])
    T = Sq // NR
    f = lambda a: np.ascontiguousarray(np.asarray(a, np.float32))
    colp, grow = pack_params(DEPTH, f(norm_g), f(mem_norm_g), f(conv_w), f(conv_b), f(diff_q_norm_g), f(diff_k_norm_g),
                             f(lambda_q1), f(lambda_k1), f(lambda_q2), f(lambda_k2), f(diff_head_norm_g),
                             f(mem_q_norm_g), f(mem_k_norm_g))
    key = (T, DEPTH)
    if key not in _NC_CACHE:
        _NC_CACHE[key] = build_nc(T, DEPTH)
    nc = _NC_CACHE[key]
    ident = np.eye(128, dtype=np.float32).astype(ml_dtypes.bfloat16)
    bd = np.zeros((128, 128), np.float32)
    bd[0:64, 0:64] = 1.0
    bd[64:128, 64:128] = 1.0
    w_in_f, w_mkv_f, w_out_f = f(w_in), f(w_mem_kv), f(w_out)
    in_maps = []
    for c in range(2 * NR):
        b, r = c // NR, c % NR
        hc = host_consts(T, r)
        m = dict(x=np.ascontiguousarray(x[b, r * T:(r + 1) * T]), mem=np.ascontiguousarray(mem[b]),
                 w_in=w_in_f, w_mem_kv=w_mkv_f, w_out=w_out_f, colp=colp, grow=grow,
                 ident=ident, bdones=bd)
        m.update(hc)
        in_maps.append(m)
    res = run_bass_kernel_spmd(nc, in_maps, core_ids=list(range(2 * NR)))
    out = np.zeros((Bb, Sq, D), np.float32)
    for c in range(2 * NR):
        b, r = c // NR, c % NR
        out[b, r * T:(r + 1) * T] = np.asarray(res.results[c]["out"], np.float32)
    return out
```

```python
import numpy as np
import ml_dtypes
import concourse.bass as bass
import concourse.mybir as mybir
from concourse.bass_utils import run_bass_kernel_spmd

F32 = mybir.dt.float32
BF16 = mybir.dt.bfloat16
AF = mybir.ActivationFunctionType
ALU = mybir.AluOpType
AX = mybir.AxisListType

D = 2048
KC = 16
NH = 6
NMH = 4
NMEM = 256
INC = 7168
EPS = 1e-6
NR = 4
NEG = -30000.0
ENG = ("sp", "act", "dve", "pool", "pe")
C_NG, C_MNG, C_W0, C_W1, C_W2, C_CB, C_GQ, C_GK, C_GH, C_GMQ, C_GMK, C_LAM = 0, 16, 32, 38, 44, 50, 56, 57, 58, 59, 60, 61
NCOLP = 65


class _I:
    def __getattr__(self, name):
        def f(*a, **k):
            return (name, a, k)
        return f


I = _I()


class Buf:
    def __init__(self, name, dsem=None):
        self.name = name
        self.w = None
        self.r = {}
        self.dsem = dsem
        self.dcnt = 0


class Sch:
    def __init__(self, nc):
        self.nc = nc
        self.q = {e: [] for e in ENG}
        self.cnt = {e: 0 for e in ENG}
        self.sem = {e: nc.alloc_semaphore("es_" + e) for e in ENG}
        self.seen = {e: {} for e in ENG}
        self.nsem = 5
        self.free = []
        self.dreg = {}

    def newsem(self, name):
        if self.free:
            return self.free.pop()
        self.nsem += 1
        sem = self.nc.alloc_semaphore("d%d" % self.nsem)
        self.dreg[id(sem)] = [sem, 0]
        return sem

    def relsem(self, sem):
        self.free.append(sem)

    def barrier(self):
        tks = [(self.sem[e], self.cnt[e]) for e in ENG if self.cnt[e] > 0]
        tks += [(sv[0], sv[1]) for sv in self.dreg.values() if sv[1] > 0]
        for e in ENG:
            for tk in tks:
                self._wait(e, tk)

    def _wait(self, e, tk):
        if tk is None:
            return
        sem, val = tk
        if e == "pe" and sem is self.sem["pe"]:
            return
        key = id(sem)
        if self.seen[e].get(key, 0) >= val:
            return
        self.seen[e][key] = val
        self.q[e].append(("w", sem, val))

    def _deps(self, e, reads, writes):
        for b in reads:
            self._wait(e, b.w)
        for b in writes:
            self._wait(e, b.w)
            for tk in b.r.values():
                self._wait(e, tk)

    def _commit(self, tk, reads, writes):
        for b in reads:
            b.r[id(tk[0])] = tk
        for b in writes:
            b.w = tk
            b.r = {}

    def op(self, e, fns, reads=(), writes=()):
        if isinstance(fns, tuple):
            fns = [fns]
        self._deps(e, reads, writes)
        self.cnt[e] += 1
        tk = (self.sem[e], self.cnt[e])
        for f in fns[:-1]:
            self.q[e].append(("i", f, None))
        self.q[e].append(("i", fns[-1], (self.sem[e], 1)))
        self._commit(tk, reads, writes)
        return tk

    def dma(self, e, fn, sb, reads=(), writes=()):
        self._deps(e, reads, writes)
        rec = self.dreg[id(sb.dsem)]
        rec[1] += 16
        tk = (sb.dsem, rec[1])
        self.q[e].append(("i", fn, (sb.dsem, 16)))
        self._commit(tk, reads, writes)
        return tk

    def emit(self, e, eng):
        for it in self.q[e]:
            if it[0] == "w":
                eng.wait_ge(it[1], it[2])
            else:
                name, a, k = it[1]
                ins = getattr(eng, name)(*a, **k)
                if it[2] is not None:
                    ins.then_inc(it[2][0], it[2][1])


DEBUG = False


def build_nc(T, DEPTH):
    NQT = T // 512
    TB = T // 128
    NTT = T // 128
    TW = 896
    TOFF = 384
    nc = bass.Bass("TRN2", target_bir_lowering=False)
    S = Sch(nc)

    def din(name, shape, dt=F32):
        return nc.dram_tensor(name, list(shape), dt, kind="ExternalInput").ap()

    x_in = din("x", [T, D])
    mem_in = din("mem", [NMEM, D])
    w_in = din("w_in", [DEPTH, D, INC])
    w_mkv = din("w_mem_kv", [DEPTH, D, 1024])
    w_out = din("w_out", [DEPTH, D, D])
    colp_in = din("colp", [DEPTH, 128, NCOLP])
    grow_in = din("grow", [DEPTH, 2, D])
    bcol_in = din("bcol", [128, NR * NH * NQT * TB])
    atab_in = din("atab", [128, NR * 3 * NH * 128], BF16)
    brow_in = din("brow", [128, 512], BF16)
    toep_in = din("toep", [128, TW])
    hsel_in = din("hsel", [128, 96])
    slp_in = din("slp", [128, NH])
    ident_in = din("ident", [128, 128], BF16)
    bd_in = din("bdones", [128, 128])
    out_d = nc.dram_tensor("out", [T, D], F32, kind="ExternalOutput").ap()

    xbuf = nc.dram_tensor("xbuf", [T, D], F32).ap()
    k_loc = nc.dram_tensor("k_loc", [NH * 128, T], BF16).ap()
    k_all = [nc.dram_tensor("k_all%d" % h, [NR * 128, T], BF16).ap() for h in range(NH)]
    v_loc = [nc.dram_tensor("v_loc%d" % h, [T, 128], BF16).ap() for h in range(NH)]
    v_all = [nc.dram_tensor("v_all%d" % h, [NR * T, 128], BF16).ap() for h in range(NH)]
    ue_loc = nc.dram_tensor("ue_loc", [2, 768], F32).ap()
    ue_all = nc.dram_tensor("ue_all", [2 * NR, 768], F32).ap()
    yT_scr = nc.dram_tensor("yT_scr", [16 * 128, T], BF16).ap()
    sgd_scr = nc.dram_tensor("sgd_scr", [NH * 128, T], F32).ap()
    if DEBUG:
        dbg_y = nc.dram_tensor("dbg_y", [16 * 128, T], BF16, kind="ExternalOutput").ap()

    B_xbuf = [Buf("xbuf%d" % i) for i in range(NTT)]
    B_kloc = [Buf("kloc%d" % i) for i in range(NH)]
    B_vloc = [Buf("vloc%d" % i) for i in range(NH)]
    B_kall = [Buf("kall%d" % i) for i in range(NH)]
    B_vall = [Buf("vall%d" % i) for i in range(NH)]
    B_ueloc = Buf("ueloc")
    B_ueall = Buf("ueall")
    B_yT = [Buf("yT%d" % i) for i in range(16)]
    B_sgd = [Buf("sgd%d" % i) for i in range(NH)]

    uid = {"n": 0}

    class Scope:
        def __init__(self):
            self.guards = []
            self.sems = []

        def close(self):
            S.barrier()
            for g in reversed(self.guards):
                g.__exit__(None, None, None)
            for sm in self.sems:
                S.relsem(sm)

    class Tile:
        def __init__(self, name, shape, dt, dma=False, scope=None):
            uid["n"] += 1
            nm = "%s_u%d" % (name, uid["n"])
            if scope is None:
                self.t = nc.alloc_sbuf_tensor(nm, list(shape), dt)
            else:
                g = nc.sbuf_tensor(nm, list(shape), dt)
                self.t = g.__enter__()
                scope.guards.append(g)
            sem = S.newsem(nm) if dma else None
            if sem is not None and scope is not None:
                scope.sems.append(sem)
            self.b = Buf(nm, sem)

    def pool(name, n, shape, dt, dma=False, scope=None):
        tiles = [Tile("%s%d" % (name, i), shape, dt, dma, scope) for i in range(n)]
        state = {"i": 0}

        def nxt():
            t = tiles[state["i"] % n]
            state["i"] += 1
            return t
        return nxt

    ident = Tile("ident", [128, 128], BF16, True)
    onesf = Tile("onesf", [128, 128], F32)
    bdf = Tile("bdf", [128, 128], F32, True)
    onesb = Tile("onesb", [128, 128], BF16)
    brow = Tile("brow", [128, 512], BF16, True)
    hsel = Tile("hsel", [128, 96], F32, True)
    slp = Tile("slp", [128, NH], F32, True)
    colp = [Tile("colp%d" % l, [128, NCOLP], F32, True) for l in range(DEPTH)]
    derv = [Tile("derv%d" % l, [128, 8], F32) for l in range(DEPTH)]
    for tl, src in ((ident, ident_in), (bdf, bd_in), (brow, brow_in), (hsel, hsel_in), (slp, slp_in)):
        S.dma("sp", I.dma_start(out=tl.t[:], in_=src), tl.b, writes=[tl.b])
    for l in range(DEPTH):
        S.dma("sp", I.dma_start(out=colp[l].t[:], in_=colp_in[l]), colp[l].b, writes=[colp[l].b])
    S.op("dve", I.memset(onesf.t[:], 1.0), writes=[onesf.b])
    S.op("dve", I.memset(onesb.t[:], 1.0), writes=[onesb.b])

    ps = nc.alloc_psum_tensor("ps", [128, 4096], F32)
    PB = [Buf("pb%d" % i) for i in range(8)]
    pst = {"c": 0}

    def nb():
        b = pst["c"] % 8
        pst["c"] += 1
        return b

    def nb2():
        if pst["c"] % 2:
            pst["c"] += 1
        b = pst["c"] % 8
        pst["c"] += 2
        return b

    def bank(b, n=1):
        return ps[:, b * 512:(b + n) * 512]

    tmp = pool("tmp", 10, [128, 512], F32)
    small = pool("small", 8, [128, 16], F32)
    ezg = Tile("ezg", [128, 24], F32)
    yedge = Tile("yedge", [128, 6, 2], BF16, True)
    mkT = Tile("mkT", [128, NMH, NMEM], BF16)
    mv = Tile("mv", [128, 2, 512], BF16)

    def rsqrt_chain(src_ap, src_bufs, mean_scale, width):
        t = tmp()
        S.op("dve", I.tensor_scalar(out=t.t[:, 0:width], in0=src_ap, scalar1=mean_scale, scalar2=EPS,
                                              op0=ALU.mult, op1=ALU.add), reads=src_bufs, writes=[t.b])
        S.op("act", I.activation(out=t.t[:, 0:width], in_=t.t[:, 0:width], func=AF.Ln),
             reads=[t.b], writes=[t.b])
        S.op("act", I.activation(out=t.t[:, 0:width], in_=t.t[:, 0:width], func=AF.Exp, scale=-0.5),
             reads=[t.b], writes=[t.b])
        return t

    def silu_from_psum(b, width=512):
        t = tmp()
        S.op("act", I.activation(out=t.t[:, 0:width], in_=bank(b)[:, 0:width], func=AF.Exp, scale=-1.0),
             reads=[PB[b]], writes=[t.b])
        S.op("dve", I.tensor_scalar(out=t.t[:, 0:width], in0=t.t[:, 0:width], scalar1=1.0, scalar2=None,
                                              op0=ALU.add), reads=[t.b], writes=[t.b])
        S.op("dve", I.reciprocal(out=t.t[:, 0:width], in_=t.t[:, 0:width]), reads=[t.b], writes=[t.b])
        S.op("dve", I.tensor_tensor(out=t.t[:, 0:width], in0=bank(b)[:, 0:width], in1=t.t[:, 0:width],
                                              op=ALU.mult), reads=[t.b, PB[b]], writes=[t.b])
        return t

    for l in range(DEPTH):
        lam_init = 0.8 - 0.6 * float(np.exp(-0.3 * l))
        cp = colp[l]
        dv = derv[l]
        last = (l == DEPTH - 1)
        x_src = x_in if l == 0 else xbuf
        x_dst = out_d if last else xbuf

        S.op("dve", I.tensor_scalar(out=dv.t[:, 0:1], in0=cp.t[:, C_GQ:C_GQ + 1], scalar1=0.125, scalar2=None,
                                              op0=ALU.mult), reads=[cp.b], writes=[dv.b])
        S.op("dve", I.tensor_scalar(out=dv.t[:, 2:3], in0=cp.t[:, C_GH:C_GH + 1], scalar1=float(1.0 - lam_init),
                                              scalar2=None, op0=ALU.mult), reads=[cp.b, dv.b], writes=[dv.b])
        S.op("dve", I.tensor_scalar(out=dv.t[:, 3:4], in0=cp.t[:, C_GMQ:C_GMQ + 1], scalar1=float(128 ** -0.5),
                                              scalar2=None, op0=ALU.mult), reads=[cp.b, dv.b], writes=[dv.b])
        lt = small()
        S.op("dve", I.tensor_tensor(out=lt.t[:, 0:1], in0=cp.t[:, C_LAM:C_LAM + 1], in1=cp.t[:, C_LAM + 1:C_LAM + 2],
                                              op=ALU.mult), reads=[cp.b], writes=[lt.b])
        S.op("dve", I.tensor_tensor(out=lt.t[:, 1:2], in0=cp.t[:, C_LAM + 2:C_LAM + 3], in1=cp.t[:, C_LAM + 3:C_LAM + 4],
                                              op=ALU.mult), reads=[cp.b, lt.b], writes=[lt.b])
        b = nb()
        S.op("pe", I.matmul(bank(b)[:, 0:2], onesf.t[:], lt.t[:, 0:2], start=True, stop=True),
             reads=[onesf.b, lt.b], writes=[PB[b]])
        S.op("act", I.activation(out=lt.t[:, 2:4], in_=bank(b)[:, 0:2], func=AF.Exp), reads=[PB[b], lt.b], writes=[lt.b])
        S.op("dve", I.tensor_tensor(out=lt.t[:, 4:5], in0=lt.t[:, 3:4], in1=lt.t[:, 2:3], op=ALU.subtract),
             reads=[lt.b], writes=[lt.b])
        S.op("dve", I.tensor_scalar(out=dv.t[:, 4:5], in0=lt.t[:, 4:5], scalar1=float(-lam_init), scalar2=None,
                                              op0=ALU.add), reads=[lt.b, dv.b], writes=[dv.b])

        sc_PA = Scope()
        qT = [Tile("qT%d" % h, [128, T], BF16, False, sc_PA) for h in range(NH)]
        yst = pool("yst", 2, [128, T], BF16, True, sc_PA)
        sc_NP = Scope()
        hT = Tile("hT_%d" % l, [128, KC, T], BF16, False, sc_NP)
        mnT = Tile("mnT_%d" % l, [128, KC, NMEM], BF16, False, sc_NP)
        sc_N = Scope()
        grow = [Tile("grow%d_%d" % (i, l), [128, D], F32, True, sc_N) for i in range(2)]
        hn_pool = pool("hn_%d" % l, 2, [128, D], BF16, False, sc_N)
        sq_pool = pool("sqj_%d" % l, 2, [128, D], F32, False, sc_N)
        xt_pool = pool("xt", 3, [128, D], F32, True, sc_N)
        for i in range(2):
            S.dma("sp", I.dma_start(out=grow[i].t[:], in_=grow_in[l, i:i + 1, :].partition_broadcast(128)),
                  grow[i].b, writes=[grow[i].b])

        def norm_tile(src_ap, src_bufs, gtile, dstT, col0):
            xt = xt_pool()
            sq_t = sq_pool()
            S.dma("sp", I.dma_start(out=xt.t[:], in_=src_ap), xt.b, reads=src_bufs, writes=[xt.b])
            S.op("act", I.activation(out=sq_t.t[:], in_=xt.t[:], func=AF.Square), reads=[xt.b], writes=[sq_t.b])
            st = small()
            S.op("dve", I.reduce_sum(out=st.t[:, 0:1], in_=sq_t.t[:], axis=AX.X), reads=[sq_t.b], writes=[st.b])
            S.op("dve", I.tensor_scalar(out=st.t[:, 1:2], in0=st.t[:, 0:1], scalar1=1.0 / D, scalar2=EPS,
                                                  op0=ALU.mult, op1=ALU.add), reads=[st.b], writes=[st.b])
            S.op("act", I.activation(out=st.t[:, 2:3], in_=st.t[:, 1:2], func=AF.Ln), reads=[st.b], writes=[st.b])
            S.op("act", I.activation(out=st.t[:, 3:4], in_=st.t[:, 2:3], func=AF.Exp, scale=-0.5), reads=[st.b], writes=[st.b])
            hn = hn_pool()
            S.op("dve", I.scalar_tensor_tensor(out=hn.t[:], in0=xt.t[:], scalar=st.t[:, 3:4], in1=gtile.t[:],
                                                         op0=ALU.mult, op1=ALU.mult), reads=[xt.b, st.b, gtile.b], writes=[hn.b])
            for half in range(2):
                b = nb()
                pv = bank(b).bitcast(BF16)
                S.op("pe", [I.transpose(out=pv[:, j * 128:(j + 1) * 128],
                                                                          in_=hn.t[:, (half * 8 + j) * 128:(half * 8 + j + 1) * 128],
                                                                          identity=ident.t[:]) for j in range(8)],
                     reads=[hn.b, ident.b], writes=[PB[b]])
                S.op("act", I.activation(
                    out=dstT.t[:, half * 8:(half + 1) * 8, col0:col0 + 128],
                    in_=pv.rearrange("p (j c) -> p j c", c=128), func=AF.Copy), reads=[PB[b]], writes=[dstT.b])

        for tt in range(NTT):
            norm_tile(x_src[tt * 128:(tt + 1) * 128, :], [B_xbuf[tt]] if l > 0 else [], grow[0], hT, tt * 128)
        for tt in range(2):
            norm_tile(mem_in[tt * 128:(tt + 1) * 128, :], [], grow[1], mnT, tt * 128)

        sc_N.close()
        sc_P = Scope()
        wpool = pool("wc_%d" % l, 6, [128, KC, 128], BF16, True, sc_P)

        def load_w(src2d, c0):
            w = wpool()
            S.dma("pool", I.dma_start(out=w.t[:], in_=src2d.rearrange("(kc p) c -> p kc c", p=128)[:, :, c0:c0 + 128]),
                  w.b, writes=[w.b])
            return w

        jobs = []
        for m in range(NMH):
            jobs.append(("mk", m, m))
        for m in range(NMH):
            jobs.append(("mv", m, NMH + m))
        for h in range(NH):
            jobs.append(("k", h, 30 + h))
        for h in range(NH):
            jobs.append(("v", h, 36 + h))
        for ci in range(6):
            jobs.append(("ax", ci, ci))
            jobs.append(("ac", ci, 12 + ci))
            jobs.append(("ab", ci, 6 + ci))
            jobs.append(("ag", ci, 18 + ci))
        for h in range(NH):
            jobs.append(("q", h, 24 + h))
        for h in range(NH):
            jobs.append(("dg", h, 42 + h))
        for m in range(NMH):
            jobs.append(("mq", m, 48 + m))
            jobs.append(("mg", m, 52 + m))
        wq = {}
        PF = 4

        def ensure_loaded(i):
            for j in range(0, min(i + PF + 1, len(jobs))):
                if j not in wq:
                    kind, a, c = jobs[j]
                    src = w_mkv[l] if kind in ("mk", "mv") else w_in[l]
                    wq[j] = load_w(src, c * 128)

        def proj_fm(w, rhsT, col0, width):
            b = nb()
            S.op("pe", [I.matmul(bank(b)[:, 0:width], w.t[:, kc, :], rhsT.t[:, kc, col0:col0 + width],
                                                       start=(kc == 0), stop=(kc == KC - 1)) for kc in range(KC)],
                 reads=[w.b, rhsT.b], writes=[PB[b]])
            return b

        def qk_norm(b, width, ones_t, mean_scale, gcol_ap, gbufs, out_ap, out_bufs):
            xs = tmp()
            sq = tmp()
            S.op("act", I.activation(out=xs.t[:, 0:width], in_=bank(b)[:, 0:width], func=AF.Copy), reads=[PB[b]], writes=[xs.b])
            S.op("act", I.activation(out=sq.t[:, 0:width], in_=bank(b)[:, 0:width], func=AF.Square), reads=[PB[b]], writes=[sq.b])
            b2 = nb()
            S.op("pe", I.matmul(bank(b2)[:, 0:width], ones_t.t[:], sq.t[:, 0:width], start=True, stop=True),
                 reads=[ones_t.b, sq.b], writes=[PB[b2]])
            rs = rsqrt_chain(bank(b2)[:, 0:width], [PB[b2]], mean_scale, width)
            S.op("dve", I.scalar_tensor_tensor(out=out_ap, in0=xs.t[:, 0:width], scalar=gcol_ap, in1=rs.t[:, 0:width],
                                                         op0=ALU.mult, op1=ALU.mult), reads=[xs.b, rs.b] + gbufs, writes=out_bufs)

        u_pool = pool("u_%d" % l, 1, [128, T + 2], F32, True, sc_P)
        for _ in range(1):
            ut = u_pool()
            S.op("dve", I.memset(ut.t[:], 0.0), writes=[ut.b])
        vst = Tile("vst_%d" % l, [128, NTT, 128], BF16, True, sc_P)
        kst_pool = pool("kst_%d" % l, 2, [128, T], BF16, True, sc_P)
        sgst_pool = pool("sgst_%d" % l, 1, [128, T], F32, True, sc_P)
        om_pool = pool("om_%d" % l, 1, [128, T], F32, False, sc_P)
        pm_pool = pool("pm_%d" % l, 2, [128, 1024], BF16, False, sc_P)
        mqn_pool = pool("mqn_%d" % l, 2, [128, 512], BF16, False, sc_P)
        u_cur = None
        om_cur = None
        ensure_loaded(0)
        for ji, (kind, a, c) in enumerate(jobs):
            if ji > 0:
                ensure_loaded(ji)
            w = wq[ji]
            if kind == "mk":
                b = proj_fm(w, mnT, 0, NMEM)
                qk_norm(b, NMEM, onesf, 1.0 / 128, cp.t[:, C_GMK:C_GMK + 1], [cp.b], mkT.t[:, a, :], [mkT.b])
            elif kind == "mv":
                for mb in range(2):
                    b = nb()
                    S.op("pe", [I.matmul(bank(b)[:, 0:128], mnT.t[:, kc, mb * 128:(mb + 1) * 128], w.t[:, kc, :],
                                                                       start=(kc == 0), stop=(kc == KC - 1)) for kc in range(KC)],
                         reads=[w.b, mnT.b], writes=[PB[b]])
                    S.op("act", I.activation(out=mv.t[:, mb, a * 128:(a + 1) * 128], in_=bank(b)[:, 0:128], func=AF.Copy),
                         reads=[PB[b]], writes=[mv.b])
            elif kind == "k":
                kst = kst_pool()
                for qt in range(NQT):
                    b = proj_fm(w, hT, qt * 512, 512)
                    qk_norm(b, 512, bdf, 1.0 / 64, cp.t[:, C_GK:C_GK + 1], [cp.b], kst.t[:, qt * 512:(qt + 1) * 512], [kst.b])
                S.dma("sp", I.dma_start(out=k_loc[a * 128:(a + 1) * 128, :], in_=kst.t[:]), kst.b,
                      reads=[kst.b], writes=[B_kloc[a]])
                S.op("pool", I.collective_compute("AllGather", ALU.bypass, replica_groups=[[0, 1, 2, 3], [4, 5, 6, 7]],
                                                  ins=[k_loc[a * 128:(a + 1) * 128, :].opt()], outs=[k_all[a].opt()]),
                     reads=[B_kloc[a]], writes=[B_kall[a]])
            elif kind == "v":
                for t4 in range(NTT // 4):
                    b = nb()
                    fns = []
                    for j in range(4):
                        tt = t4 * 4 + j
                        for kc in range(KC):
                            fns.append(I.matmul(bank(b)[:, j * 128:(j + 1) * 128],
                                                                                  hT.t[:, kc, tt * 128:(tt + 1) * 128], w.t[:, kc, :],
                                                                                  start=(kc == 0), stop=(kc == KC - 1)))
                    S.op("pe", fns, reads=[w.b, hT.b], writes=[PB[b]])
                    S.op("act", I.activation(out=vst.t[:, t4 * 4:(t4 + 1) * 4, :],
                                                                   in_=bank(b).rearrange("p (j c) -> p j c", c=128), func=AF.Copy),
                         reads=[PB[b]], writes=[vst.b])
                S.dma("sp", I.dma_start(out=v_loc[a].rearrange("(tt p) c -> p tt c", p=128), in_=vst.t[:]), vst.b,
                      reads=[vst.b], writes=[B_vloc[a]])
                S.op("pool", I.collective_compute("AllGather", ALU.bypass, replica_groups=[[0, 1, 2, 3], [4, 5, 6, 7]],
                                                  ins=[v_loc[a].opt()], outs=[v_all[a].opt()]),
                     reads=[B_vloc[a]], writes=[B_vall[a]])
            elif kind == "ax":
                u_cur = u_pool()
                for qt in range(NQT):
                    b = proj_fm(w, hT, qt * 512, 512)
                    S.op("act", I.activation(out=u_cur.t[:, 1 + qt * 512:1 + (qt + 1) * 512], in_=bank(b), func=AF.Copy),
                         reads=[PB[b]], writes=[u_cur.b])
            elif kind == "ac":
                u = u_cur
                for qt in range(NQT):
                    b = proj_fm(w, hT, qt * 512, 512)
                    S.op("dve", I.tensor_tensor(out=u.t[:, 1 + qt * 512:1 + (qt + 1) * 512],
                                                                               in0=bank(b), in1=u.t[:, 1 + qt * 512:1 + (qt + 1) * 512], op=ALU.mult),
                         reads=[PB[b], u_cur.b], writes=[u_cur.b])
                S.dma("sp", I.dma_start(out=ue_loc[0:1, a * 128:(a + 1) * 128].rearrange("o p -> p o"), in_=u.t[:, 1:2], allow_slow_non_contiguous=True),
                      u_cur.b, reads=[u_cur.b], writes=[B_ueloc])
                S.dma("sp", I.dma_start(out=ue_loc[1:2, a * 128:(a + 1) * 128].rearrange("o p -> p o"), in_=u.t[:, T:T + 1], allow_slow_non_contiguous=True),
                      u_cur.b, reads=[u_cur.b], writes=[B_ueloc])
            elif kind == "ab":
                ab_w = w
            elif kind == "ag":
                y = yst()
                for qt in range(NQT):
                    bb = proj_fm(ab_w, hT, qt * 512, 512)
                    bg = proj_fm(w, hT, qt * 512, 512)
                    sg = silu_from_psum(bg)
                    S.op("dve", I.tensor_tensor(out=sg.t[:], in0=bank(bb), in1=sg.t[:], op=ALU.mult),
                         reads=[PB[bb], sg.b], writes=[sg.b])
                    z = tmp()
                    u = u_cur
                    o0 = qt * 512
                    S.op("dve", I.tensor_scalar(out=z.t[:], in0=u.t[:, o0:o0 + 512], scalar1=cp.t[:, C_W0 + a:C_W0 + a + 1],
                                                                                scalar2=cp.t[:, C_CB + a:C_CB + a + 1], op0=ALU.mult, op1=ALU.add),
                         reads=[u.b, cp.b], writes=[z.b])
                    S.op("dve", I.scalar_tensor_tensor(out=z.t[:], in0=u.t[:, o0 + 1:o0 + 513], scalar=cp.t[:, C_W1 + a:C_W1 + a + 1],
                                                                                       in1=z.t[:], op0=ALU.mult, op1=ALU.add),
                         reads=[u.b, cp.b, z.b], writes=[z.b])
                    S.op("dve", I.scalar_tensor_tensor(out=z.t[:], in0=u.t[:, o0 + 2:o0 + 514], scalar=cp.t[:, C_W2 + a:C_W2 + a + 1],
                                                                                       in1=z.t[:], op0=ALU.mult, op1=ALU.add),
                         reads=[u.b, cp.b, z.b], writes=[z.b])
                    S.op("dve", I.tensor_tensor(out=y.t[:, o0:o0 + 512], in0=z.t[:], in1=sg.t[:], op=ALU.mult),
                         reads=[z.b, sg.b], writes=[y.b])
                    if qt == 0:
                        S.op("dve", I.tensor_copy(out=ezg.t[:, 2 * a:2 * a + 1], in_=z.t[:, 0:1]), reads=[z.b, ezg.b], writes=[ezg.b])
                        S.op("dve", I.tensor_copy(out=ezg.t[:, 12 + 2 * a:12 + 2 * a + 1], in_=sg.t[:, 0:1]), reads=[sg.b, ezg.b], writes=[ezg.b])
                    if qt == NQT - 1:
                        S.op("dve", I.tensor_copy(out=ezg.t[:, 2 * a + 1:2 * a + 2], in_=z.t[:, 511:512]), reads=[z.b, ezg.b], writes=[ezg.b])
                        S.op("dve", I.tensor_copy(out=ezg.t[:, 12 + 2 * a + 1:12 + 2 * a + 2], in_=sg.t[:, 511:512]), reads=[sg.b, ezg.b], writes=[ezg.b])
                S.dma("sp", I.dma_start(out=yT_scr[a * 128:(a + 1) * 128, :], in_=y.t[:]), y.b, reads=[y.b], writes=[B_yT[a]])
                if a == 5:
                    S.op("pool", I.collective_compute("AllGather", ALU.bypass, replica_groups=[[0, 1, 2, 3], [4, 5, 6, 7]],
                                                                ins=[ue_loc.opt()], outs=[ue_all.opt()]),
                         reads=[B_ueloc], writes=[B_ueall])
            elif kind == "q":
                for qt in range(NQT):
                    b = proj_fm(w, hT, qt * 512, 512)
                    qk_norm(b, 512, bdf, 1.0 / 64, dv.t[:, 0:1], [dv.b], qT[a].t[:, qt * 512:(qt + 1) * 512], [qT[a].b])
            elif kind == "dg":
                sgs = sgst_pool()
                for qt in range(NQT):
                    b = proj_fm(w, hT, qt * 512, 512)
                    sg = silu_from_psum(b)
                    S.op("act", I.activation(out=sgs.t[:, qt * 512:(qt + 1) * 512], in_=sg.t[:], func=AF.Copy),
                         reads=[sg.b], writes=[sgs.b])
                S.dma("sp", I.dma_start(out=sgd_scr[a * 128:(a + 1) * 128, :], in_=sgs.t[:]), sgs.b,
                      reads=[sgs.b], writes=[B_sgd[a]])
            elif kind == "mq":
                om_cur = om_pool()
                om = om_cur
                for qt in range(NQT):
                    b = proj_fm(w, hT, qt * 512, 512)
                    mqn = mqn_pool()
                    qk_norm(b, 512, onesf, 1.0 / 128, dv.t[:, 3:4], [dv.b], mqn.t[:], [mqn.b])
                    b2 = nb2()
                    for blk in range(2):
                        S.op("pe", I.matmul(bank(b2 + blk), mkT.t[:, a, blk * 128:(blk + 1) * 128], mqn.t[:],
                                                                               start=True, stop=True),
                             reads=[mkT.b, mqn.b], writes=[PB[b2 + blk]])
                    pm = pm_pool()
                    S.op("act", I.activation(out=pm.t[:], in_=bank(b2, 2), func=AF.Exp),
                         reads=[PB[b2], PB[b2 + 1]], writes=[pm.b])
                    bo = nb()
                    bd_ = nb()
                    S.op("pe", [I.matmul(bank(bo), mv.t[:, blk, a * 128:(a + 1) * 128], pm.t[:, blk * 512:(blk + 1) * 512],
                                                                          start=(blk == 0), stop=(blk == 1)) for blk in range(2)],
                         reads=[mv.b, pm.b], writes=[PB[bo]])
                    S.op("pe", [I.matmul(bank(bd_), onesb.t[:], pm.t[:, blk * 512:(blk + 1) * 512],
                                                                            start=(blk == 0), stop=(blk == 1)) for blk in range(2)],
                         reads=[onesb.b, pm.b], writes=[PB[bd_]])
                    rd = tmp()
                    S.op("dve", I.reciprocal(out=rd.t[:], in_=bank(bd_)), reads=[PB[bd_]], writes=[rd.b])
                    S.op("dve", I.tensor_tensor(out=om.t[:, qt * 512:(qt + 1) * 512], in0=bank(bo), in1=rd.t[:], op=ALU.mult),
                         reads=[PB[bo], rd.b], writes=[om_cur.b])
            elif kind == "mg":
                y = yst()
                om = om_cur
                for qt in range(NQT):
                    b = proj_fm(w, hT, qt * 512, 512)
                    sg = silu_from_psum(b)
                    S.op("dve", I.tensor_tensor(out=y.t[:, qt * 512:(qt + 1) * 512], in0=om.t[:, qt * 512:(qt + 1) * 512],
                                                                                        in1=sg.t[:], op=ALU.mult),
                         reads=[sg.b, om_cur.b], writes=[y.b])
                S.dma("sp", I.dma_start(out=yT_scr[(12 + a) * 128:(13 + a) * 128, :], in_=y.t[:]), y.b, reads=[y.b], writes=[B_yT[12 + a]])

        ue = Tile("ue_%d" % l, [128, 8, 6], F32, True, sc_P)
        S.dma("sp", I.dma_start(out=ue.t[:], in_=ue_all.rearrange("r (c p) -> p r c", p=128), allow_slow_non_contiguous=True),
              ue.b, reads=[B_ueall], writes=[ue.b])
        hl = small()
        ut = Tile("uet_%d" % l, [128, 6, 8], F32, False, sc_P)
        for side in range(2):
            S.op("dve", I.tensor_tensor(out=ut.t[:], in0=ue.t[:].rearrange("p r c -> p c r"), in1=hsel.t[:, side * 48:(side + 1) * 48].rearrange("p (c r) -> p c r", r=8),
                                                            op=ALU.mult), reads=[ue.b, hsel.b], writes=[ut.b])
            S.op("dve", I.reduce_sum(out=hl.t[:, side * 6:(side + 1) * 6], in_=ut.t[:], axis=AX.X), reads=[ut.b, hl.b], writes=[hl.b])
            wc = C_W0 if side == 0 else C_W2
            S.op("dve", I.tensor_tensor(out=hl.t[:, side * 6:(side + 1) * 6], in0=hl.t[:, side * 6:(side + 1) * 6],
                                                                    in1=cp.t[:, wc:wc + 6], op=ALU.mult), reads=[hl.b, cp.b], writes=[hl.b])
            ezv = ezg.t[:, 0:12].rearrange("p (c s) -> p c s", s=2)
            egv = ezg.t[:, 12:24].rearrange("p (c s) -> p c s", s=2)
            S.op("dve", I.tensor_tensor(out=hl.t[:, side * 6:(side + 1) * 6], in0=hl.t[:, side * 6:(side + 1) * 6],
                                                                      in1=ezv[:, :, side], op=ALU.add), reads=[hl.b, ezg.b], writes=[hl.b])
            S.op("dve", I.tensor_tensor(out=yedge.t[:, :, side], in0=hl.t[:, side * 6:(side + 1) * 6],
                                                                      in1=egv[:, :, side], op=ALU.mult), reads=[hl.b, ezg.b], writes=[yedge.b])
        yv = yT_scr.rearrange("(c p) t -> p c t", p=128)
        S.dma("sp", I.dma_start(out=yv[:, 0:6, 0:1], in_=yedge.t[:, :, 0:1], allow_slow_non_contiguous=True), yedge.b, reads=[yedge.b], writes=B_yT[0:6])
        S.dma("sp", I.dma_start(out=yv[:, 0:6, T - 1:T], in_=yedge.t[:, :, 1:2], allow_slow_non_contiguous=True), yedge.b, reads=[yedge.b], writes=B_yT[0:6])

        sc_P.close()
        sc_NP.close()
        sc_A = Scope()
        NSRC = NR + 1
        bcol = Tile("bcol", [128, NR * NH * NQT * TB], F32, True, sc_A)
        atab = Tile("atab", [128, NR * 3 * NH * 128], BF16, True, sc_A)
        toep = Tile("toep", [128, TW], F32, True, sc_A)
        for tl, src in ((bcol, bcol_in), (atab, atab_in), (toep, toep_in)):
            S.dma("sp", I.dma_start(out=tl.t[:], in_=src), tl.b, writes=[tl.b])
        kb_pool = pool("kb_%d" % l, 2 * NSRC - 2, [128, T], BF16, True, sc_A)
        vb_pool = pool("vb_%d" % l, 2 * NSRC - 2, [128, TB, 128], BF16, True, sc_A)
        sgd_pool = pool("sgdp_%d" % l, 2, [128, T], F32, True, sc_A)
        p_pool = pool("pt_%d" % l, 5, [128, 1024], BF16, False, sc_A)
        ssb_pool = pool("ssb_%d" % l, 2, [128, 1024], F32, False, sc_A)
        dacc_pool = pool("dacc_%d" % l, 2, [128, 1024], F32, False, sc_A)
        pending = []
        O1, O2, D1, D2 = 4, 5, 6, 7

        def load_src(h, s):
            kt = kb_pool()
            vt = vb_pool()
            if s < NR:
                ksrc = k_all[h][s * 128:(s + 1) * 128, :]
                vsrc = v_all[h][s * T:(s + 1) * T, :].rearrange("(kb p) d -> p kb d", p=128)
                rb_k, rb_v = [B_kall[h]], [B_vall[h]]
            else:
                ksrc = k_loc[h * 128:(h + 1) * 128, :]
                vsrc = v_loc[h].rearrange("(kb p) d -> p kb d", p=128)
                rb_k, rb_v = [B_kloc[h]], [B_vloc[h]]
            S.dma("sp", I.dma_start(out=kt.t[:], in_=ksrc), kt.b, reads=rb_k, writes=[kt.b])
            S.dma("sp", I.dma_start(out=vt.t[:], in_=vsrc), vt.b, reads=rb_v, writes=[vt.b])
            return (kt, vt)

        def load_sgd(h):
            sg = sgd_pool()
            S.dma("sp", I.dma_start(out=sg.t[:], in_=sgd_scr[h * 128:(h + 1) * 128, :]), sg.b, reads=[B_sgd[h]], writes=[sg.b])
            return sg

        NEARLY = NSRC - 2
        cur_srcs = [load_src(0, s) for s in range(NSRC)]
        cur_sgd = load_sgd(0)
        pst["c"] = 0
        for h in range(NH):
            srcs, sgd = cur_srcs, cur_sgd
            if h + 1 < NH:
                nxt_srcs = [load_src(h + 1, s) for s in range(NEARLY)]
                nxt_sgd = load_sgd(h + 1)
            y = yst()
            for qt in range(NQT):
                q0 = qt * 512
                blocks = [(s, kb) for s in range(NR) for kb in range(TB)] + [(NR, kb) for kb in range(4 * qt, 4 * qt + 4)]
                nblk = len(blocks)
                dacc = dacc_pool()
                pe_d = [bi for bi in range(nblk) if bi % 6 == 5 and bi >= 11]
                pe_d_first = pe_d[0] if pe_d else -1

                def emit_S(bi):
                    s, kb = blocks[bi]
                    kt, vt = srcs[s]
                    b2 = (bi % 2) * 2
                    pt = p_pool()
                    if s < NR:
                        cls = 0 if kb < 4 * qt else (1 if kb < 4 * qt + 4 else 2)
                        acol = ((s * 3 + cls) * NH + h) * 128
                        bidx = ((s * NH + h) * NQT + qt) * TB + kb
                        fns = []
                        for hf in range(2):
                            fns.append(I.matmul(bank(b2 + hf), kt.t[hf * 64:(hf + 1) * 64, kb * 128:(kb + 1) * 128],
                                                qT[h].t[hf * 64:(hf + 1) * 64, q0:q0 + 512], start=True, stop=False))
                        for hf in range(2):
                            fns.append(I.matmul(bank(b2 + hf), atab.t[64 * hf:64 * hf + 64, acol:acol + 128], brow.t[64 * hf:64 * hf + 64, :],
                                                start=False, stop=True))
                        S.op("pe", fns, reads=[kt.b, qT[h].b, atab.b, brow.b], writes=[PB[b2], PB[b2 + 1]])
                        S.op("act", I.activation(out=pt.t[:], in_=bank(b2, 2), func=AF.Exp, bias=bcol.t[:, bidx:bidx + 1]),
                             reads=[PB[b2], PB[b2 + 1], bcol.b], writes=[pt.b])
                    else:
                        fns = []
                        for hf in range(2):
                            fns.append(I.matmul(bank(b2 + hf), kt.t[hf * 64:(hf + 1) * 64, kb * 128:(kb + 1) * 128],
                                                qT[h].t[hf * 64:(hf + 1) * 64, q0:q0 + 512], start=True, stop=True))
                        S.op("pe", fns, reads=[kt.b, qT[h].b], writes=[PB[b2], PB[b2 + 1]])
                        ssb = ssb_pool()
                        m0 = TOFF - 128 * (kb - 4 * qt)
                        for hf in range(2):
                            S.op("dve", I.scalar_tensor_tensor(
                                out=ssb.t[:, hf * 512:(hf + 1) * 512], in0=toep.t[:, m0:m0 + 512], scalar=slp.t[:, h:h + 1], in1=bank(b2 + hf),
                                op0=ALU.mult, op1=ALU.add), reads=[toep.b, slp.b, PB[b2 + hf], ssb.b], writes=[ssb.b])
                        S.op("act", I.activation(out=pt.t[:], in_=ssb.t[:], func=AF.Exp), reads=[ssb.b], writes=[pt.b])
                    return pt

                def emit_PV(bi, pt):
                    s, kb = blocks[bi]
                    kt, vt = srcs[s]
                    first = (bi == 0)
                    lastb = (bi == nblk - 1)
                    fns = [I.matmul(bank(ob), vt.t[:, kb, :], pt.t[:, hf * 512:(hf + 1) * 512], start=first, stop=lastb)
                           for hf, ob in enumerate((O1, O2))]
                    S.op("pe", fns, reads=[vt.b, pt.b], writes=[PB[O1], PB[O2]])
                    if bi in pe_d:
                        fns = [I.matmul(bank(db), onesb.t[:], pt.t[:, hf * 512:(hf + 1) * 512], start=(bi == pe_d_first), stop=False)
                               for hf, db in enumerate((D1, D2))]
                        S.op("pe", fns, reads=[onesb.b, pt.b], writes=[PB[D1], PB[D2]])
                    elif first:
                        S.op("dve", I.tensor_copy(out=dacc.t[:], in_=pt.t[:]), reads=[pt.b], writes=[dacc.b])
                    else:
                        S.op("dve", I.tensor_tensor(out=dacc.t[:], in0=dacc.t[:], in1=pt.t[:], op=ALU.add), reads=[pt.b, dacc.b], writes=[dacc.b])

                pts = {}
                for bi in range(nblk):
                    pts[bi] = emit_S(bi)
                    if bi >= 2:
                        emit_PV(bi - 2, pts.pop(bi - 2))
                    if bi == 3 and pending:
                        pending.pop()()
                emit_PV(nblk - 2, pts.pop(nblk - 2))
                emit_PV(nblk - 1, pts.pop(nblk - 1))
                o1s = tmp()
                o2s = tmp()
                S.op("dve", I.tensor_copy(out=o1s.t[:], in_=bank(O1)), reads=[PB[O1]], writes=[o1s.b])
                S.op("dve", I.tensor_copy(out=o2s.t[:], in_=bank(O2)), reads=[PB[O2]], writes=[o2s.b])

                def epilogue_b(o1s=o1s, o2s=o2s, dacc=dacc, y=y, q0=q0, sgd=sgd, has_pe_d=bool(pe_d)):
                    for hf, db in enumerate((D1, D2)):
                        S.op("pe", I.matmul(bank(db), onesf.t[:], dacc.t[:, hf * 512:(hf + 1) * 512], start=(not has_pe_d), stop=True),
                             reads=[onesf.b, dacc.b], writes=[PB[db]])
                    r1 = tmp()
                    r2 = tmp()
                    for rr_, db in ((r1, D1), (r2, D2)):
                        S.op("act", I.activation(out=rr_.t[:], in_=bank(db), func=AF.Ln), reads=[PB[db]], writes=[rr_.b])
                        S.op("act", I.activation(out=rr_.t[:], in_=rr_.t[:], func=AF.Exp, scale=-1.0), reads=[rr_.b], writes=[rr_.b])
                    S.op("dve", I.tensor_tensor(out=o1s.t[:], in0=o1s.t[:], in1=r1.t[:], op=ALU.mult), reads=[o1s.b, r1.b], writes=[o1s.b])
                    S.op("dve", I.tensor_tensor(out=r2.t[:], in0=o2s.t[:], in1=r2.t[:], op=ALU.mult), reads=[o2s.b, r2.b], writes=[r2.b])
                    S.op("dve", I.scalar_tensor_tensor(out=o1s.t[:], in0=r2.t[:], scalar=dv.t[:, 4:5], in1=o1s.t[:], op0=ALU.mult, op1=ALU.add),
                         reads=[r2.b, o1s.b, dv.b], writes=[o1s.b])
                    sq = tmp()
                    S.op("act", I.activation(out=sq.t[:], in_=o1s.t[:], func=AF.Square), reads=[o1s.b], writes=[sq.b])
                    S.op("pe", I.matmul(bank(D1), onesf.t[:], sq.t[:], start=True, stop=True), reads=[onesf.b, sq.b], writes=[PB[D1]])
                    rs = rsqrt_chain(bank(D1), [PB[D1]], 1.0 / 128, 512)
                    S.op("dve", I.scalar_tensor_tensor(out=o1s.t[:], in0=o1s.t[:], scalar=dv.t[:, 2:3], in1=rs.t[:], op0=ALU.mult, op1=ALU.mult),
                         reads=[o1s.b, rs.b, dv.b], writes=[o1s.b])
                    S.op("dve", I.tensor_tensor(out=y.t[:, q0:q0 + 512], in0=o1s.t[:], in1=sgd.t[:, q0:q0 + 512], op=ALU.mult),
                         reads=[o1s.b, sgd.b], writes=[y.b])

                if qt == NQT - 1:
                    epilogue_b()
                else:
                    pending.append(epilogue_b)
            S.dma("sp", I.dma_start(out=yT_scr[(6 + h) * 128:(7 + h) * 128, :], in_=y.t[:]), y.b, reads=[y.b], writes=[B_yT[6 + h]])
            if h + 1 < NH:
                nxt_srcs += [load_src(h + 1, s) for s in range(NEARLY, NSRC)]
                cur_srcs, cur_sgd = nxt_srcs, nxt_sgd

        sc_A.close()
        sc_PA.close()
        if DEBUG and l == 0:
            S.dma("sp", I.dma_start(out=dbg_y, in_=yT_scr), ident.b, reads=B_yT, writes=[])
            S.barrier()
        sc_O = Scope()
        wo = [Tile("wo%d_%d" % (cg, l), [128, KC, 512], BF16, True, sc_O) for cg in range(4)]
        for cg in range(4):
            S.dma("pool", I.dma_start(out=wo[cg].t[:], in_=w_out[l].rearrange("(kc p) c -> p kc c", p=128)[:, :, cg * 512:(cg + 1) * 512]),
                  wo[cg].b, writes=[wo[cg].b])
        yg_pool = pool("yg_%d" % l, 2, [128, KC, 512], BF16, True, sc_O)
        xo_pool = pool("xo_%d" % l, 2, [128, D], F32, True, sc_O)
        yvv = yT_scr.rearrange("(c p) t -> p c t", p=128)
        pst["c"] = 0
        for g in range(NQT):
            yg = yg_pool()
            S.dma("sp", I.dma_start(out=yg.t[:], in_=yvv[:, :, g * 512:(g + 1) * 512]), yg.b, reads=B_yT, writes=[yg.b])
            for t4 in range(4):
                tt = g * 4 + t4
                xo = xo_pool()
                S.dma("sp", I.dma_start(out=xo.t[:], in_=x_src[tt * 128:(tt + 1) * 128, :]), xo.b,
                      reads=[B_xbuf[tt]] if l > 0 else [], writes=[xo.b])
                for cg in range(4):
                    b = nb()
                    S.op("pe", [I.matmul(bank(b), yg.t[:, kc, t4 * 128:(t4 + 1) * 128], wo[cg].t[:, kc, :],
                                                                                    start=(kc == 0), stop=(kc == KC - 1)) for kc in range(KC)],
                         reads=[yg.b, wo[cg].b], writes=[PB[b]])
                    S.op("dve", I.tensor_tensor(out=xo.t[:, cg * 512:(cg + 1) * 512], in0=bank(b), in1=xo.t[:, cg * 512:(cg + 1) * 512],
                                                                            op=ALU.add), reads=[PB[b], xo.b], writes=[xo.b])
                S.dma("sp", I.dma_start(out=x_dst[tt * 128:(tt + 1) * 128, :], in_=xo.t[:]), xo.b,
                      reads=[xo.b], writes=[B_xbuf[tt]])

        sc_O.close()

    S.barrier()

    with nc.Block() as block:
        @block.sync
        def _(e):
            S.emit("sp", e)

        @block.scalar
        def _(e):
            S.emit("act", e)

        @block.vector
        def _(e):
            S.emit("dve", e)

        @block.gpsimd
        def _(e):
            S.emit("pool", e)

        @block.tensor
        def _(e):
            S.emit("pe", e)
    return nc


def _bf16_parts(v, n=3):
    parts = []
    r = np.float64(v)
    for _ in range(n):
        p = np.float32(r).astype(ml_dtypes.bfloat16)
        parts.append(p)
        r = r - np.float64(p.astype(np.float32))
    return parts


def host_consts(T, r):
    NQT = T // 512
    TB = T // 128
    slopes = np.array([2.0 ** (-8.0 * (h + 1) / NH) for h in range(NH)], dtype=np.float32)
    jj = np.arange(128, dtype=np.float64)
    bcol = np.zeros((128, NR, NH, NQT, TB), np.float32)
    atab = np.zeros((128, NR, 3, NH, 128), dtype=ml_dtypes.bfloat16)
    for s in range(NR):
        for h in range(NH):
            sl = np.float64(slopes[h])
            for qt in range(NQT):
                i0 = r * T + 512 * qt
                for kb in range(TB):
                    j = s * T + 128 * kb + jj
                    if s == r and 4 * qt <= kb < 4 * qt + 4:
                        bcol[:, s, h, qt, kb] = NEG
                        continue
                    sig = 1.0 if (s < r or (s == r and kb < 4 * qt)) else -1.0
                    bcol[:, s, h, qt, kb] = (sig * sl * (j - i0)).astype(np.float32)
            for cls in range(3):
                if s == r:
                    sig = 1.0 if cls == 0 else -1.0
                else:
                    sig = 1.0 if s < r else -1.0
                parts = _bf16_parts(-sig * sl)
                for a in range(3):
                    for b in range(2):
                        atab[a * 2 + b, s, cls, h, :] = parts[a]
                        atab[64 + a * 2 + b, s, cls, h, :] = parts[a]
    ii = np.arange(512)
    brow = np.zeros((128, 512), dtype=ml_dtypes.bfloat16)
    hi = (ii - (ii % 2)).astype(np.float32)
    lo = (ii % 2).astype(np.float32)
    for a in range(3):
        for o in (0, 64):
            brow[o + a * 2 + 0] = hi.astype(ml_dtypes.bfloat16)
            brow[o + a * 2 + 1] = lo.astype(ml_dtypes.bfloat16)
    TW = 896
    TOFF = 384
    m = np.arange(TW, dtype=np.float64)[None, :]
    toep = (-np.abs(m - TOFF - jj[:, None])).astype(np.float32)
    hsel = np.zeros((128, 2, 6, 8), np.float32)
    if r > 0:
        hsel[:, 0, :, (r - 1) * 2 + 1] = 1.0
    if r < NR - 1:
        hsel[:, 1, :, (r + 1) * 2 + 0] = 1.0
    slp = np.tile(slopes[None, :], (128, 1)).astype(np.float32)
    return dict(bcol=bcol.reshape(128, -1), atab=atab.reshape(128, -1), brow=brow, toep=toep,
                hsel=hsel.reshape(128, 96), slp=slp)


def pack_params(DEPTH, norm_g, mem_norm_g, conv_w, conv_b, diff_q_norm_g, diff_k_norm_g, lambda_q1, lambda_k1,
                lambda_q2, lambda_k2, diff_head_norm_g, mem_q_norm_g, mem_k_norm_g):
    colp = np.zeros((DEPTH, 128, NCOLP), np.float32)
    for l in range(DEPTH):
        colp[l, :, C_NG:C_NG + 16] = norm_g[l].reshape(16, 128).T
        colp[l, :, C_MNG:C_MNG + 16] = mem_norm_g[l].reshape(16, 128).T
        for k, c0 in ((0, C_W0), (1, C_W1), (2, C_W2)):
            colp[l, :, c0:c0 + 6] = conv_w[l, k].reshape(6, 128).T
        colp[l, :, C_CB:C_CB + 6] = conv_b[l].reshape(6, 128).T
        colp[l, :, C_GQ] = np.tile(diff_q_norm_g[l], 2)
        colp[l, :, C_GK] = np.tile(diff_k_norm_g[l], 2)
        colp[l, :, C_GH] = diff_head_norm_g[l]
        colp[l, :, C_GMQ] = mem_q_norm_g[l]
        colp[l, :, C_GMK] = mem_k_norm_g[l]
        colp[l, 0:64, C_LAM + 0] = lambda_q1[l]
        colp[l, 0:64, C_LAM + 1] = lambda_k1[l]
        colp[l, 0:64, C_LAM + 2] = lambda_q2[l]
        colp[l, 0:64, C_LAM + 3] = lambda_k2[l]
    grow = np.stack([norm_g, mem_norm_g], axis=1).astype(np.float32)
    return colp, grow


_NC_CACHE = {}


def kernel(x, mem, norm_g, w_in, conv_w, conv_b, diff_q_norm_g, diff_k_norm_g,
           lambda_q1, lambda_k1, lambda_q2, lambda_k2, diff_head_norm_g,
           mem_norm_g, w_mem_kv, mem_q_norm_g, mem_k_norm_g, w_out):
    x = np.asarray(x, np.float32)
    mem = np.asarray(mem, np.float32)
    Bb, Sq, _ = x.shape
    DEPTH = int(np.asarray(w_in).shape[0])
    T = Sq // NR
    f = lambda a: np.ascontiguousarray(np.asarray(a, np.float32))
    colp, grow = pack_params(DEPTH, f(norm_g), f(mem_norm_g), f(conv_w), f(conv_b), f(diff_q_norm_g), f(diff_k_norm_g),
                             f(lambda_q1), f(lambda_k1), f(lambda_q2), f(lambda_k2), f(diff_head_norm_g),
                             f(mem_q_norm_g), f(mem_k_norm_g))
    key = (T, DEPTH)
    if key not in _NC_CACHE:
        _NC_CACHE[key] = build_nc(T, DEPTH)
    nc = _NC_CACHE[key]
    ident = np.eye(128, dtype=np.float32).astype(ml_dtypes.bfloat16)
    bd = np.zeros((128, 128), np.float32)
    bd[0:64, 0:64] = 1.0
    bd[64:128, 64:128] = 1.0
    w_in_f, w_mkv_f, w_out_f = f(w_in), f(w_mem_kv), f(w_out)
    in_maps = []
    for c in range(2 * NR):
        b, r = c // NR, c % NR
        hc = host_consts(T, r)
        m = dict(x=np.ascontiguousarray(x[b, r * T:(r + 1) * T]), mem=np.ascontiguousarray(mem[b]),
                 w_in=w_in_f, w_mem_kv=w_mkv_f, w_out=w_out_f, colp=colp, grow=grow,
                 ident=ident, bdones=bd)
        m.update(hc)
        in_maps.append(m)
    res = run_bass_kernel_spmd(nc, in_maps, core_ids=list(range(2 * NR)))
    out = np.zeros((Bb, Sq, D), np.float32)
    for c in range(2 * NR):
        b, r = c // NR, c % NR
        out[b, r * T:(r + 1) * T] = np.asarray(res.results[c]["out"], np.float32)
    return out
```
